# Optimizing a Trainium2 kernel written in Bass

```python
import jax
import jax.numpy as jnp
from jax import lax
import numpy as np

D_MODEL = 2048
BATCH = 8
SEQ = 4096
DEPTH = 4

HEAD_DIM = 128
BRANCH_WIDTH = D_MODEL // 2
N_BRANCH = 3
PLE_DIM = 256
NORM_EPS = 1e-6

LRU_BLOCK = 64
LRU_BLOCKS = BRANCH_WIDTH // LRU_BLOCK
CONV_WIDTH = 4
LRU_C = 8.0

NSA_HEADS = BRANCH_WIDTH // HEAD_DIM
NSA_KV_HEADS = 2
NSA_GROUP = NSA_HEADS // NSA_KV_HEADS
NSA_KV_WIDTH = NSA_KV_HEADS * HEAD_DIM
CMP_BLOCK = 32
CMP_STRIDE = 16
CMP_HIDDEN = 256
SEL_BLOCK = 64
SEL_TOPK = 16
WINDOW = 512
NSA_Q_BLOCK = 32

FOX_HEADS = BRANCH_WIDTH // HEAD_DIM
FOX_Q_BLOCK = 128
FORGET_BIAS = 3.0

IN_SPLITS = (
    ('lru_x', BRANCH_WIDTH), ('lru_gate', BRANCH_WIDTH),
    ('nsa_q', BRANCH_WIDTH),
    ('nsa_k_cmp', NSA_KV_WIDTH), ('nsa_v_cmp', NSA_KV_WIDTH),
    ('nsa_k_slc', NSA_KV_WIDTH), ('nsa_v_slc', NSA_KV_WIDTH),
    ('nsa_k_win', NSA_KV_WIDTH), ('nsa_v_win', NSA_KV_WIDTH),
    ('nsa_bgate', NSA_HEADS * 3), ('nsa_gate', BRANCH_WIDTH),
    ('fox_q', BRANCH_WIDTH), ('fox_k', BRANCH_WIDTH), ('fox_v', BRANCH_WIDTH),
    ('fox_f', FOX_HEADS), ('fox_gate', BRANCH_WIDTH),
    ('merge', N_BRANCH * D_MODEL),
)
N_IN = sum(size for _, size in IN_SPLITS)

kernel_name = 'hybrid_rglru_nsa_fox_block'


def column_offsets():
    offs, start = {}, 0
    for name, size in IN_SPLITS:
        offs[name] = (start, size)
        start += size
    return offs


def split_columns(z):
    return {name: z[..., s:s + n] for name, (s, n) in column_offsets().items()}


def split_heads(t, n_heads):
    b, s, _ = t.shape
    return t.reshape(b, s, n_heads, HEAD_DIM)


def rms_norm(x, gain):
    xf = x.astype(jnp.float32)
    y = xf * lax.rsqrt(jnp.mean(xf * xf, axis=-1, keepdims=True) + NORM_EPS)
    return (y * gain.astype(jnp.float32)).astype(x.dtype)


def masked_softmax(logits, mask):
    logits = jnp.where(mask, logits.astype(jnp.float32), -jnp.inf)
    m = jnp.max(logits, axis=-1, keepdims=True)
    m = jnp.where(jnp.isfinite(m), m, 0.0)
    e = jnp.exp(logits - m)
    return e / jnp.maximum(jnp.sum(e, axis=-1, keepdims=True), 1e-30)


def linear_combine(left, right):
    a_l, b_l = left
    a_r, b_r = right
    return a_l * a_r, a_r * b_l + b_r


def rglru_branch(u, conv_w, conv_b, wa, ba, wx, bx, lam):
    b, s, w = u.shape
    taps = conv_w[:, None, :].astype(u.dtype)
    uc = lax.conv_general_dilated(u, taps, (1,), ((CONV_WIDTH - 1, 0),),
                                  dimension_numbers=('NWC', 'WIO', 'NWC'),
                                  feature_group_count=w) + conv_b
    ub = uc.reshape(b, s, LRU_BLOCKS, LRU_BLOCK)
    rec = jax.nn.sigmoid((jnp.einsum('bsnk,nkj->bsnj', ub, wa).reshape(b, s, w) + ba).astype(jnp.float32))
    inp = jax.nn.sigmoid((jnp.einsum('bsnk,nkj->bsnj', ub, wx).reshape(b, s, w) + bx).astype(jnp.float32))
    log_a = -LRU_C * rec * jax.nn.softplus(-lam.astype(jnp.float32))
    a = jnp.exp(log_a)
    drive = jnp.sqrt(-jnp.expm1(2.0 * log_a)) * (inp * uc.astype(jnp.float32))
    _, hseq = lax.associative_scan(linear_combine, (a, drive), axis=1)
    return hseq.astype(u.dtype)


def compress(t, pos, w1, w2):
    b, s, hk, dk = t.shape
    n_chunk = s // CMP_STRIDE
    ratio = CMP_BLOCK // CMP_STRIDE
    n_cmp = n_chunk - ratio + 1
    chunks = t.reshape(b, n_chunk, CMP_STRIDE, hk, dk)
    blocks = jnp.concatenate([chunks[:, r:r + n_cmp] for r in range(ratio)], axis=2)
    blocks = blocks + pos[:, None, :]
    flat = blocks.transpose(0, 1, 3, 2, 4).reshape(b, n_cmp, hk, CMP_BLOCK * dk)
    return jax.nn.silu(flat @ w1) @ w2


def nsa_branch(q, kc, vc, ks, vs, kw, vw, gates):
    b, s, h, dk = q.shape
    scale = dk ** -0.5
    n_cmp = kc.shape[1]
    n_sel = s // SEL_BLOCK
    n_top = min(SEL_TOPK, n_sel)
    nb = s // NSA_Q_BLOCK
    cmp_start = jnp.arange(n_cmp) * CMP_STRIDE
    cmp_end = cmp_start + CMP_BLOCK - 1
    sel_ids = jnp.arange(n_sel)
    sel_start = sel_ids * SEL_BLOCK
    overlap = ((cmp_start[:, None] < sel_start[None, :] + SEL_BLOCK)
               & (cmp_start[:, None] + CMP_BLOCK > sel_start[None, :])).astype(jnp.float32)
    ks_blk = ks.reshape(b, n_sel, SEL_BLOCK, NSA_KV_HEADS, dk).transpose(0, 3, 1, 2, 4)
    vs_blk = vs.reshape(b, n_sel, SEL_BLOCK, NSA_KV_HEADS, dk).transpose(0, 3, 1, 2, 4)
    bi = jnp.arange(b)[:, None, None, None]
    gi = jnp.arange(NSA_KV_HEADS)[None, None, :, None]
    pad = ((0, 0), (WINDOW, 0), (0, 0), (0, 0))
    kw_pad = jnp.pad(kw, pad)
    vw_pad = jnp.pad(vw, pad)
    win_off = jnp.arange(WINDOW + NSA_Q_BLOCK) - WINDOW
    q_blocks = q.reshape(b, nb, NSA_Q_BLOCK, NSA_KV_HEADS, NSA_GROUP, dk).swapaxes(0, 1)
    g_blocks = gates.reshape(b, nb, NSA_Q_BLOCK, NSA_KV_HEADS, NSA_GROUP, 3).swapaxes(0, 1)
    t0s = jnp.arange(nb, dtype=jnp.int32) * NSA_Q_BLOCK
    flat_sel = n_top * SEL_BLOCK

    def block(args):
        qb, gb, t0 = args
        tq = t0 + jnp.arange(NSA_Q_BLOCK)
        s_c = jnp.einsum('btgqd,bcgd->btgqc', qb, kc) * scale
        m_c = (cmp_end[None, :] <= tq[:, None])[None, :, None, None, :]
        p_c = masked_softmax(s_c, m_c)
        o_c = jnp.einsum('btgqc,bcgd->btgqd', p_c.astype(vc.dtype), vc)
        imp = jnp.einsum('btgqc,cj->btgj', p_c, overlap)
        cur = (tq // SEL_BLOCK)[:, None]
        valid = sel_ids[None, :] <= cur
        forced = (sel_ids[None, :] == 0) | (sel_ids[None, :] == cur) | (sel_ids[None, :] == cur - 1)
        score = jnp.where(forced[None, :, None, :], jnp.inf,
                          jnp.where(valid[None, :, None, :], imp, -jnp.inf))
        _, idx = lax.top_k(score, n_top)
        k_sel = ks_blk[bi, gi, idx]
        v_sel = vs_blk[bi, gi, idx]
        s_s = jnp.einsum('btgqd,btgnkd->btgqnk', qb, k_sel) * scale
        kpos = idx[..., None] * SEL_BLOCK + jnp.arange(SEL_BLOCK)
        m_s = kpos <= tq[None, :, None, None, None]
        p_s = masked_softmax(s_s.reshape(b, NSA_Q_BLOCK, NSA_KV_HEADS, NSA_GROUP, flat_sel),
                             m_s.reshape(b, NSA_Q_BLOCK, NSA_KV_HEADS, 1, flat_sel))
        o_s = jnp.einsum('btgqm,btgmd->btgqd', p_s.astype(vs.dtype),
                         v_sel.reshape(b, NSA_Q_BLOCK, NSA_KV_HEADS, flat_sel, dk))
        k_w = lax.dynamic_slice_in_dim(kw_pad, t0, WINDOW + NSA_Q_BLOCK, axis=1)
        v_w = lax.dynamic_slice_in_dim(vw_pad, t0, WINDOW + NSA_Q_BLOCK, axis=1)
        wpos = t0 + win_off
        m_w = ((wpos[None, :] >= 0) & (wpos[None, :] <= tq[:, None])
               & (wpos[None, :] > tq[:, None] - WINDOW))[None, :, None, None, :]
        s_w = jnp.einsum('btgqd,bkgd->btgqk', qb, k_w) * scale
        p_w = masked_softmax(s_w, m_w)
        o_w = jnp.einsum('btgqk,bkgd->btgqd', p_w.astype(vw.dtype), v_w)
        return gb[..., 0:1] * o_c + gb[..., 1:2] * o_s + gb[..., 2:3] * o_w

    out = lax.map(block, (q_blocks, g_blocks, t0s))
    return out.swapaxes(0, 1).reshape(b, s, h * dk)


def fox_branch(q, k, v, log_f):
    b, s, h, dk = q.shape
    scale = dk ** -0.5
    nb = s // FOX_Q_BLOCK
    c = jnp.cumsum(log_f, axis=1)
    ck = c.transpose(0, 2, 1)
    kpos = jnp.arange(s)
    q_blocks = q.reshape(b, nb, FOX_Q_BLOCK, h, dk).swapaxes(0, 1)
    c_blocks = c.reshape(b, nb, FOX_Q_BLOCK, h).swapaxes(0, 1)
    t0s = jnp.arange(nb, dtype=jnp.int32) * FOX_Q_BLOCK

    def block(args):
        qb, cb, t0 = args
        tq = t0 + jnp.arange(FOX_Q_BLOCK)
        logits = jnp.einsum('bqhd,bkhd->bhqk', qb, k).astype(jnp.float32) * scale
        logits = logits + (cb.transpose(0, 2, 1)[..., None] - ck[:, :, None, :])
        p_att = masked_softmax(logits, kpos[None, :] <= tq[:, None])
        return jnp.einsum('bhqk,bkhd->bqhd', p_att.astype(v.dtype), v)

    out = lax.map(block, (q_blocks, c_blocks, t0s))
    return out.swapaxes(0, 1).reshape(b, s, h * dk)


def setup_inputs(seed: int = 0) -> dict:
    key = jax.random.key(seed)
    ks = jax.random.split(key, 26)
    f32 = jnp.float32

    def nrm(k, shape, fan_in):
        return jax.random.normal(k, shape, f32) * (fan_in ** -0.5)

    def gain(k, shape):
        return 1.0 + 0.05 * jax.random.normal(k, shape, f32)

    def small(k, shape):
        return 0.02 * jax.random.normal(k, shape, f32)

    x = jax.random.normal(ks[0], (BATCH, SEQ, D_MODEL), f32)
    p = jax.random.normal(ks[1], (DEPTH, BATCH, SEQ, PLE_DIM), f32)
    ln_gain = gain(ks[2], (DEPTH, D_MODEL))
    w_in = nrm(ks[3], (DEPTH, D_MODEL, N_IN), D_MODEL)
    f_off = column_offsets()['fox_f'][0]
    b_in = small(ks[4], (DEPTH, N_IN)).at[:, f_off:f_off + FOX_HEADS].add(FORGET_BIAS)
    conv_w = nrm(ks[5], (DEPTH, CONV_WIDTH, BRANCH_WIDTH), CONV_WIDTH)
    conv_b = small(ks[6], (DEPTH, BRANCH_WIDTH))
    lru_wa = nrm(ks[7], (DEPTH, LRU_BLOCKS, LRU_BLOCK, LRU_BLOCK), LRU_BLOCK)
    lru_ba = small(ks[8], (DEPTH, BRANCH_WIDTH))
    lru_wx = nrm(ks[9], (DEPTH, LRU_BLOCKS, LRU_BLOCK, LRU_BLOCK), LRU_BLOCK)
    lru_bx = small(ks[10], (DEPTH, BRANCH_WIDTH))
    a0 = jax.random.uniform(ks[11], (DEPTH, BRANCH_WIDTH), f32, minval=0.9, maxval=0.999)
    root = a0 ** (1.0 / LRU_C)
    lru_lambda = jnp.log(root) - jnp.log1p(-root)
    cmp_w1 = nrm(ks[12], (DEPTH, 2, CMP_BLOCK * HEAD_DIM, CMP_HIDDEN), CMP_BLOCK * HEAD_DIM)
    cmp_w2 = nrm(ks[13], (DEPTH, 2, CMP_HIDDEN, HEAD_DIM), CMP_HIDDEN)
    cmp_pos = small(ks[14], (DEPTH, 2, CMP_BLOCK, HEAD_DIM))
    nsa_q_gain = gain(ks[15], (DEPTH, HEAD_DIM))
    nsa_k_gain = gain(ks[16], (DEPTH, HEAD_DIM))
    fox_q_gain = gain(ks[17], (DEPTH, HEAD_DIM))
    fox_k_gain = gain(ks[18], (DEPTH, HEAD_DIM))
    w_branch = nrm(ks[19], (DEPTH, N_BRANCH, BRANCH_WIDTH, D_MODEL), BRANCH_WIDTH)
    w_out = nrm(ks[20], (DEPTH, D_MODEL, D_MODEL), D_MODEL)
    w_ple = nrm(ks[21], (DEPTH, PLE_DIM, D_MODEL), PLE_DIM)
    ple_gain = gain(ks[22], (DEPTH, D_MODEL))
    w_ple_gate = nrm(ks[23], (DEPTH, D_MODEL, D_MODEL), D_MODEL)
    return {'x': x, 'p': p, 'ln_gain': ln_gain, 'w_in': w_in, 'b_in': b_in,
            'conv_w': conv_w, 'conv_b': conv_b, 'lru_wa': lru_wa, 'lru_ba': lru_ba,
            'lru_wx': lru_wx, 'lru_bx': lru_bx, 'lru_lambda': lru_lambda,
            'cmp_w1': cmp_w1, 'cmp_w2': cmp_w2, 'cmp_pos': cmp_pos,
            'nsa_q_gain': nsa_q_gain, 'nsa_k_gain': nsa_k_gain,
            'fox_q_gain': fox_q_gain, 'fox_k_gain': fox_k_gain,
            'w_branch': w_branch, 'w_out': w_out, 'w_ple': w_ple,
            'ple_gain': ple_gain, 'w_ple_gate': w_ple_gate}


def reference(x, p, ln_gain, w_in, b_in, conv_w, conv_b, lru_wa, lru_ba, lru_wx, lru_bx,
              lru_lambda, cmp_w1, cmp_w2, cmp_pos, nsa_q_gain, nsa_k_gain, fox_q_gain,
              fox_k_gain, w_branch, w_out, w_ple, ple_gain, w_ple_gate):
    b, s, _ = x.shape
    h = x
    for l in range(DEPTH):
        hn = rms_norm(h, ln_gain[l])
        z = split_columns(hn @ w_in[l] + b_in[l])
        y_a = rglru_branch(z['lru_x'], conv_w[l], conv_b[l], lru_wa[l], lru_ba[l],
                           lru_wx[l], lru_bx[l], lru_lambda[l]) * jax.nn.silu(z['lru_gate'])
        q_b = rms_norm(split_heads(z['nsa_q'], NSA_HEADS), nsa_q_gain[l])
        k_c = rms_norm(compress(split_heads(z['nsa_k_cmp'], NSA_KV_HEADS), cmp_pos[l, 0],
                                cmp_w1[l, 0], cmp_w2[l, 0]), nsa_k_gain[l])
        v_c = compress(split_heads(z['nsa_v_cmp'], NSA_KV_HEADS), cmp_pos[l, 1],
                       cmp_w1[l, 1], cmp_w2[l, 1])
        k_s = rms_norm(split_heads(z['nsa_k_slc'], NSA_KV_HEADS), nsa_k_gain[l])
        v_s = split_heads(z['nsa_v_slc'], NSA_KV_HEADS)
        k_w = rms_norm(split_heads(z['nsa_k_win'], NSA_KV_HEADS), nsa_k_gain[l])
        v_w = split_heads(z['nsa_v_win'], NSA_KV_HEADS)
        g_b = jax.nn.sigmoid(z['nsa_bgate']).reshape(b, s, NSA_HEADS, 3)
        y_b = nsa_branch(q_b, k_c, v_c, k_s, v_s, k_w, v_w, g_b) * jax.nn.silu(z['nsa_gate'])
        log_f = jax.nn.log_sigmoid(z['fox_f'].astype(jnp.float32))
        y_c = fox_branch(rms_norm(split_heads(z['fox_q'], FOX_HEADS), fox_q_gain[l]),
                         rms_norm(split_heads(z['fox_k'], FOX_HEADS), fox_k_gain[l]),
                         split_heads(z['fox_v'], FOX_HEADS), log_f) * jax.nn.silu(z['fox_gate'])
        mg = jax.nn.sigmoid(z['merge']).reshape(b, s, N_BRANCH, D_MODEL)
        merged = (mg[:, :, 0] * (y_a @ w_branch[l, 0])
                  + mg[:, :, 1] * (y_b @ w_branch[l, 1])
                  + mg[:, :, 2] * (y_c @ w_branch[l, 2]))
        h = h + merged @ w_out[l]
        e = rms_norm(p[l] @ w_ple[l], ple_gain[l])
        h = h + jax.nn.sigmoid(h @ w_ple_gate[l]) * e
    return h
```

```python
import numpy as np
import concourse.bass as bass
import concourse.mybir as mybir
from concourse.bass_utils import run_bass_kernel_spmd
from contextlib import ExitStack

F32 = mybir.dt.float32
BF16 = mybir.dt.bfloat16
AF = mybir.ActivationFunctionType
ALU = mybir.AluOpType
AX = mybir.AxisListType

ENGS = ("pe", "act", "dve", "pool", "sp")


class Buf:
    __slots__ = ("name", "last_w", "readers", "sem", "sem_sw", "dma_count")

    def __init__(self, name):
        self.name = name
        self.last_w = None
        self.readers = []
        self.sem = None
        self.sem_sw = None
        self.dma_count = 0


class Prog:
    def __init__(self, nc, n_dma_sems=150):
        self.nc = nc
        self.ops = {e: [] for e in ENGS}
        self.nsig = {e: 0 for e in ENGS}
        self.dma_sem_total = {}
        self.next_dma_sem = {"h": 0, "s": 0}
        self.phase_base = {"h": 0, "s": 0}
        self.max_dma_sem = {"h": 0, "s": 0}
        self.nbar = 0

    def freeze_persistent(self):
        self.phase_base = dict(self.next_dma_sem)

    def new_phase(self):
        self.barrier()
        for k in "hs":
            self.max_dma_sem[k] = max(self.max_dma_sem[k], self.next_dma_sem[k])
        self.next_dma_sem = dict(self.phase_base)

    def _deps_for(self, eng, reads, writes, is_dma=False):
        deps = []
        for b in reads:
            d = b.last_w
            if d is not None and not (d[0] == "c" and d[1] == eng and eng == "pe"):
                deps.append(d)
        for b in writes:
            for d in [b.last_w] + b.readers:
                if d is None:
                    continue
                if d[0] == "c" and d[1] == eng and eng == "pe" and not is_dma:
                    continue
                deps.append(d)
        return deps

    def op(self, eng, fn, reads=(), writes=(), sig=True):
        deps = self._deps_for(eng, reads, writes)
        idx = len(self.ops[eng])
        ev = ("c", eng, idx)
        self.ops[eng].append(dict(fn=fn, deps=deps, sig=sig, dma=None))
        for b in reads:
            b.readers.append(ev)
        for b in writes:
            b.last_w = ev
            b.readers = []
        return ev

    def dma(self, queue, fn, sbuf, reads=(), writes=()):
        deps = self._deps_for(queue, reads, writes, is_dma=True)
        attr = "sem_sw" if queue == "pool" else "sem"
        kind = "s" if queue == "pool" else "h"
        if getattr(sbuf, attr) is None:
            setattr(sbuf, attr, (kind, self.next_dma_sem[kind]))
            self.next_dma_sem[kind] += 1
        si = getattr(sbuf, attr)
        self.dma_sem_total[si] = self.dma_sem_total.get(si, 0) + 16
        ev = ("d", si, self.dma_sem_total[si])
        self.ops[queue].append(dict(fn=fn, deps=deps, sig=False, dma=si))
        for b in reads:
            b.readers.append(ev)
        for b in writes:
            b.last_w = ev
            b.readers = []
        return ev

    def barrier(self):
        self.nbar += 1
        n = self.nbar
        dma_tot = dict(self.dma_sem_total)
        marks = {e: len(self.ops[e]) for e in ENGS}
        for e in ENGS:
            self.ops[e].append(dict(bar=n, marks=marks, dma_tot=dma_tot))

    def emit(self):
        nc = self.nc
        from contextlib import ExitStack
        with ExitStack() as st:
            esem = {e: st.enter_context(nc.semaphore("cnt_" + e)) for e in ENGS}
            dsem = {}
            for k in "hs":
                n = max(self.next_dma_sem[k], self.max_dma_sem[k])
                for i in range(n):
                    dsem[(k, i)] = st.enter_context(nc.semaphore("dma%s%d" % (k, i)))
            bsem = st.enter_context(nc.semaphore("bar"))
            block = st.enter_context(nc.Block())
            for e in ENGS:
                ops = self.ops[e]
                for i, o in enumerate(ops):
                    if "bar" in o:
                        j = i - 1
                        while j >= 0 and ("bar" in ops[j] or ops[j]["dma"] is not None):
                            j -= 1
                        if j >= 0:
                            ops[j]["sig"] = True
                j = len(ops) - 1
                while j >= 0 and ("bar" in ops[j] or ops[j]["dma"] is not None):
                    j -= 1
                if j >= 0:
                    ops[j]["sig"] = True
            ticket = {}
            for e in ENGS:
                ops = self.ops[e]
                t = [0] * (len(ops) + 1)
                cnt = 0
                for i, o in enumerate(ops):
                    if o.get("sig"):
                        cnt += 1
                    t[i] = cnt
                res = [0] * len(ops)
                nxt = None
                for i in range(len(ops) - 1, -1, -1):
                    if ops[i].get("sig"):
                        nxt = t[i]
                    res[i] = nxt
                ticket[e] = res
                pre = [0] * (len(ops) + 1)
                cnt = 0
                for i, o in enumerate(ops):
                    pre[i] = cnt
                    if o.get("sig"):
                        cnt += 1
                pre[len(ops)] = cnt
                ticket[e + "_pre"] = pre

            def run(e, eng):
                waited = {}

                def wait(sem, key, val):
                    if val is None or val <= 0:
                        return
                    if waited.get(key, 0) >= val:
                        return
                    waited[key] = val
                    eng.wait_ge(sem, val)

                for o in self.ops[e]:
                    if "bar" in o:
                        n = o["bar"]
                        if e == "sp":
                            for si, tot in o["dma_tot"].items():
                                wait(dsem[si], ("d", si), tot)
                            for f in ENGS:
                                if f != "sp":
                                    wait(esem[f], ("c", f), ticket[f + "_pre"][o["marks"][f]])
                            eng.sem_inc(bsem, 1)
                        else:
                            wait(bsem, ("b",), n)
                        for si, tot in o["dma_tot"].items():
                            waited[("d", si)] = max(waited.get(("d", si), 0), tot)
                        for f in ENGS:
                            waited[("c", f)] = max(waited.get(("c", f), 0), ticket[f + "_pre"][o["marks"][f]])
                        continue
                    for d in o["deps"]:
                        if d[0] == "c":
                            wait(esem[d[1]], ("c", d[1]), ticket[d[1]][d[2]])
                        else:
                            wait(dsem[d[1]], ("d", d[1]), d[2])
                    inst = o["fn"](eng)
                    if o["dma"] is not None:
                        inst.then_inc(dsem[o["dma"]], 16)
                    elif o["sig"]:
                        inst.then_inc(esem[e], 1)

            @block.tensor
            def _(eng):
                run("pe", eng)

            @block.scalar
            def _(eng):
                run("act", eng)

            @block.vector
            def _(eng):
                run("dve", eng)

            @block.gpsimd
            def _(eng):
                run("pool", eng)

            @block.sync
            def _(eng):
                run("sp", eng)


D = 2048
S = 4096
L = 4
NIN = 15904
EPS = 1e-6
SPLITS = (('lru_x', 1024), ('lru_gate', 1024), ('nsa_q', 1024), ('nsa_k_cmp', 256), ('nsa_v_cmp', 256),
          ('nsa_k_slc', 256), ('nsa_v_slc', 256), ('nsa_k_win', 256), ('nsa_v_win', 256),
          ('nsa_bgate', 24), ('nsa_gate', 1024), ('fox_q', 1024), ('fox_k', 1024), ('fox_v', 1024),
          ('fox_f', 8), ('fox_gate', 1024), ('merge', 6144))
OFF = {}
_s = 0
for _n, _z in SPLITS:
    OFF[_n] = (_s, _z)
    _s += _z

P1_JOBS = [
    ('lru_x', 'fm', 'zlx', dict(func='ident', odt='f32')),
    ('lru_gate', 'fm', 'zlg', dict(func='silu')),
    ('nsa_q', 'hn', 'qn', dict(gain='nsa_q_gain', qscale=True)),
    ('nsa_k_cmp', 'fm', 'kci', dict(func='ident')),
    ('nsa_v_cmp', 'fm', 'vci', dict(func='ident')),
    ('nsa_k_slc', 'hn', 'ksl', dict(gain='nsa_k_gain')),
    ('nsa_v_slc', 'tm', 'vsl', dict()),
    ('nsa_k_win', 'hn', 'kwn', dict(gain='nsa_k_gain')),
    ('nsa_v_win', 'tm', 'vwn', dict()),
    ('nsa_bgate', 'tm', 'bg', dict(func='sigmoid', odt='f32')),
    ('nsa_gate', 'fm', 'ng', dict(func='silu')),
    ('fox_q', 'hn', 'fq', dict(gain='fox_q_gain', qscale=True)),
    ('fox_k', 'hn', 'fk', dict(gain='fox_k_gain')),
    ('fox_v', 'tm', 'fv', dict()),
    ('fox_f', 'fm', 'ff', dict(func='ident', odt='f32')),
    ('fox_gate', 'fm', 'fg', dict(func='silu')),
    ('merge', 'fm', 'mg', dict(func='sigmoid')),
]


def vec_layout():
    cols = {}
    n = 0

    def add(name, k):
        nonlocal n
        cols[name] = n
        n += k
    add('ln_gain', 16)
    for name, kind, dst, o in P1_JOBS:
        if kind in ('fm', 'hn'):
            add('b_' + name, (OFF[name][1] + 127) // 128)
    add('conv_w', 32)
    add('conv_b', 8)
    add('lru_ba', 8)
    add('lru_bx', 8)
    add('lru_lambda', 8)
    add('nsa_q_gain', 1)
    add('nsa_k_gain', 1)
    add('fox_q_gain', 1)
    add('fox_k_gain', 1)
    add('ple_gain', 16)
    add('cmp_pos', 64)
    return cols, n


def row_layout():
    cols = {}
    n = 0
    for name, kind, dst, o in P1_JOBS:
        if kind == 'tm':
            cols[name] = n
            n += OFF[name][1]
    return cols, n


VCOL, NV = vec_layout()
RCOL, NR = row_layout()


def host_vecs(inp):
    vec = np.zeros((L, 128, NV), np.float32)
    rows = np.zeros((L, NR), np.float32)
    for l in range(L):
        def put(name, v):
            v = np.asarray(v, np.float32).reshape(-1)
            k = (v.size + 127) // 128
            buf = np.zeros(k * 128, np.float32)
            buf[:v.size] = v
            vec[l, :, VCOL[name]:VCOL[name] + k] = buf.reshape(k, 128).T
        put('ln_gain', inp['ln_gain'][l])
        for name, kind, dst, o in P1_JOBS:
            c0, n = OFF[name]
            if kind in ('fm', 'hn'):
                put('b_' + name, inp['b_in'][l, c0:c0 + n])
            else:
                rows[l, RCOL[name]:RCOL[name] + n] = inp['b_in'][l, c0:c0 + n]
        cw = inp['conv_w'][l]
        vec[l, :, VCOL['conv_w']:VCOL['conv_w'] + 32] = cw.reshape(4, 8, 128).transpose(2, 0, 1).reshape(128, 32)
        put('conv_b', inp['conv_b'][l])
        put('lru_ba', inp['lru_ba'][l])
        put('lru_bx', inp['lru_bx'][l])
        put('lru_lambda', inp['lru_lambda'][l])
        for g in ('nsa_q_gain', 'nsa_k_gain', 'fox_q_gain', 'fox_k_gain'):
            put(g, inp[g][l])
        put('ple_gain', inp['ple_gain'][l])
        cp = inp['cmp_pos'][l]
        vec[l, :, VCOL['cmp_pos']:VCOL['cmp_pos'] + 64] = cp.reshape(64, 128).T
    return vec, rows


def host_consts():
    c = {}
    c['ident'] = np.eye(128, dtype=np.float32)
    k = np.arange(128)[:, None]
    q = np.arange(128)[None, :]
    c['tri'] = (k <= q).astype(np.float32)
    c['triu'] = (k > q).astype(np.float32)
    cs = np.arange(255) * 16
    ss = np.arange(64) * 64
    ov = ((cs[:, None] < ss[None, :] + 64) & (cs[:, None] + 32 > ss[None, :])).astype(np.float32)
    ovl = np.zeros((256, 65), np.float32)
    ovl[:255, :64] = ov
    ovl[:255, 64] = 1.0
    c['ovl'] = ovl
    cm = np.zeros((128, 17, 128), np.float32)
    for i in range(17):
        cm[:, i, :] = (128 * i + q - 16 * k >= 31)
    c['cmask'] = cm
    ql = np.arange(128)[:, None, None]
    qt = np.arange(32)[None, :, None]
    j = np.arange(64)[None, None, :]
    cur = 2 * qt + (ql >= 64)
    forced = (j == 0) | (j == cur) | (j == cur - 1)
    valid = j <= cur
    c['tkmul'] = (valid & ~forced).astype(np.float32) * np.ones((128, 32, 64), np.float32)
    c['tkadd'] = np.where(forced, 1e6 + j, np.where(valid, 0.0, -1e6 - j)).astype(np.float32) * np.ones((128, 32, 64), np.float32)
    jj = np.arange(64)[:, None, None]
    kc = np.arange(32)[None, :, None]
    kk = np.arange(128)[None, None, :]
    c['esel'] = (jj == 2 * kc + (kk >= 64)).astype(np.float32)
    return c


def MM(out, lhsT, rhs, start, stop, skip=False):
    if skip:
        return lambda e: e.matmul(out, lhsT=lhsT, rhs=rhs, start=start, stop=stop, skip_group_check=True)
    return lambda e: e.matmul(out, lhsT=lhsT, rhs=rhs, start=start, stop=stop)


def TR(out, in_, ident):
    return lambda e: e.transpose(out=out, in_=in_, identity=ident)


def ACTF(out, in_, func, bias=None, scale=None, accum_out=None):
    kw = {}
    if bias is not None:
        kw['bias'] = bias
    if scale is not None:
        kw['scale'] = scale
    if accum_out is not None:
        kw['accum_out'] = accum_out
    return lambda e: e.activation(out=out, in_=in_, func=func, **kw)


def TS(out, in0, s1, s2, op0, op1=None):
    if op1 is None:
        return lambda e: e.tensor_scalar(out=out, in0=in0, scalar1=s1, scalar2=None, op0=op0)
    return lambda e: e.tensor_scalar(out=out, in0=in0, scalar1=s1, scalar2=s2, op0=op0, op1=op1)


def STT(out, in0, scalar, in1, op0, op1):
    return lambda e: e.scalar_tensor_tensor(out=out, in0=in0, scalar=scalar, in1=in1, op0=op0, op1=op1)


def TT(out, in0, in1, op):
    return lambda e: e.tensor_tensor(out=out, in0=in0, in1=in1, op=op)


def CP(out, in_):
    return lambda e: e.tensor_copy(out=out, in_=in_)


def DMA(out, in_):
    return lambda e: e.dma_start(out=out, in_=in_)


FUNCS = {'ident': AF.Identity, 'silu': AF.Silu, 'sigmoid': AF.Sigmoid}


class RPool:
    def __init__(self, st, alloc, name, n, shape, dt):
        self.items = []
        for i in range(n):
            t = st.enter_context(alloc("%s%d" % (name, i), shape, dt))
            self.items.append((t, Buf("%s%d" % (name, i))))
        self.i = 0

    def next(self):
        it = self.items[self.i % len(self.items)]
        self.i += 1
        return it


def build(n_layers=L, dbg=(), stop_after=None, phases=None, skip=(), qts=None, fox_kw={}, p6_kw={}):
    nc = bass.Bass("TRN2", target_bir_lowering=False)
    P = Prog(nc)
    _uid = [0]

    def SBT(name, shape, dt):
        _uid[0] += 1
        return nc.sbuf_tensor("%s_%d" % (name, _uid[0]), shape, dt)

    def PST(name, shape, dt):
        _uid[0] += 1
        return nc.psum_tensor("%s_%d" % (name, _uid[0]), shape, dt)

    def din(name, shape, dt=F32):
        return nc.dram_tensor(name, list(shape), dt, kind="ExternalInput").ap()

    x = din("x", [S, D])
    p_in = din("p", [L, S, 256])
    w_in = din("w_in", [L, D, NIN])
    vec = din("vec", [L, 128, NV])
    rows = din("rows", [L, NR])
    c_ident = din("c_ident", [128, 128])
    c_tri = din("c_tri", [128, 128])
    c_triu = din("c_triu", [128, 128])
    lru_wa = din("lru_wa", [L, 16, 64, 64])
    lru_wx = din("lru_wx", [L, 16, 64, 64])
    cmp_w1 = din("cmp_w1", [L, 2, 4096, 256])
    cmp_w2 = din("cmp_w2", [L, 2, 256, 128])
    c_ovl = din("c_ovl", [256, 65])
    c_cmask = din("c_cmask", [128, 17, 128])
    c_tkmul = din("c_tkmul", [128, 32, 64])
    c_tkadd = din("c_tkadd", [128, 32, 64])
    c_esel = din("c_esel", [64, 32, 128])
    w_branch = din("w_branch", [L, 3, 16, 128, 8, 128])
    w_out = din("w_out", [L, 16, 128, 16, 128])
    w_ple = din("w_ple", [L, 256, D])
    w_ple_gate = din("w_ple_gate", [L, 16, 128, 16, 128])
    y = nc.dram_tensor("y", [S, D], F32, kind="ExternalOutput").ap()

    def scratch(name, shape, dt):
        kind = "ExternalOutput" if name in dbg else "Internal"
        return nc.dram_tensor(name, list(shape), dt, kind=kind).ap()

    SC = {}
    SC['hT'] = scratch('hT', [D, S], F32)
    SC['zlx'] = scratch('zlx', [1024, S], F32)
    SC['zlg'] = scratch('zlg', [1024, S], BF16)
    SC['qn'] = scratch('qn', [1024, S], BF16)
    SC['kci'] = scratch('kci', [256, S], BF16)
    SC['vci'] = scratch('vci', [256, S], BF16)
    SC['ksl'] = scratch('ksl', [256, S], BF16)
    SC['kwn'] = scratch('kwn', [256, S], BF16)
    SC['vsl'] = scratch('vsl', [S, 256], BF16)
    SC['vwn'] = scratch('vwn', [S, 256], BF16)
    SC['bg'] = scratch('bg', [S, 24], F32)
    SC['ng'] = scratch('ng', [1024, S], BF16)
    SC['fq'] = scratch('fq', [1024, S], BF16)
    SC['fk'] = scratch('fk', [1024, S], BF16)
    SC['fv'] = scratch('fv', [S, 1024], BF16)
    SC['ff'] = scratch('ff', [8, S], F32)
    SC['fg'] = scratch('fg', [1024, S], BF16)
    SC['mg'] = scratch('mg', [6144, S], BF16)
    SC['yaT'] = scratch('yaT', [1024, S], BF16)
    SC['ybT'] = scratch('ybT', [1024, S], BF16)
    SC['ycT'] = scratch('ycT', [1024, S], BF16)
    SC['hs'] = scratch('hs', [1024, S], F32)
    SC['cf'] = scratch('cf', [8, S], F32)
    SC['mgd'] = scratch('mgd', [D, S], BF16)
    if 'kcd' in dbg:
        SC['kcd'] = scratch('kcd', [128, 2, 256], BF16)
        SC['vcd'] = scratch('vcd', [128, 2, 2, 193], BF16)

    with ExitStack() as gst:
        def gsb(name, shape, dt):
            return gst.enter_context(SBT(name, shape, dt))
        ident_f = gsb("ident_f", [128, 128], F32)
        ident_b = gsb("ident_b", [128, 128], BF16)
        tri_b = gsb("tri_b", [128, 128], BF16)
        triu_b = gsb("triu_b", [128, 128], BF16)
        ones_b = gsb("ones_b", [128, 128], BF16)
        om128_b = gsb("om128_b", [128, 128], BF16)
        om2048_b = gsb("om2048_b", [128, 128], BF16)
        Bc = Buf("consts")
        P.dma("sp", DMA(ident_f[:], c_ident), Bc, writes=[Bc])
        P.dma("pool", DMA(ident_b[:], c_ident), Bc, writes=[Bc])
        P.dma("pool", DMA(tri_b[:], c_tri), Bc, writes=[Bc])
        P.dma("pool", DMA(triu_b[:], c_triu), Bc, writes=[Bc])
        P.op("dve", lambda e: e.memset(ones_b[:], 1.0), writes=[Bc])
        P.op("dve", lambda e: e.memset(om128_b[:], 1.0 / 128), writes=[Bc])
        P.op("dve", lambda e: e.memset(om2048_b[:], 1.0 / 2048), writes=[Bc])
        epsb = gsb("epsb", [128, 2], F32)
        P.op("dve", lambda e: e.memset(epsb[:, 0:1], EPS), writes=[Bc])
        P.op("dve", lambda e: e.memset(epsb[:, 1:2], EPS * 128.0), writes=[Bc])
        vt = gsb("vt", [128, NV], F32)
        kcT = gsb("kcT", [128, 2, 256], BF16)
        vco = gsb("vco", [128, 2, 2, 193], BF16)
        Bkc = Buf("kcT")
        Bvco = Buf("vco")
        one_f = gsb("one_f", [128, 1], F32)
        P.op("dve", lambda e: e.memset(one_f[:], 1.0), writes=[Bc])
        Bvt = Buf("vt")
        P.freeze_persistent()
        P.new_phase()

        def phase0():
            with ExitStack() as st:
                def sb(name, shape, dt):
                    return SBT(name, shape, dt)

                def ps(name, shape, dt):
                    return PST(name, shape, dt)
                xp = RPool(st, sb, "p0x", 8, [128, D], F32)
                pp = RPool(st, ps, "p0ps", 4, [128, 512], F32)
                sp_ = RPool(st, sb, "p0st", 4, [128, 512], F32)
                for g in range(8):
                    xt = []
                    for j in range(4):
                        t, B = xp.next()
                        r0 = (g * 4 + j) * 128
                        P.dma("sp", DMA(t[:], x[r0:r0 + 128, :]), B, writes=[B])
                        xt.append((t, B))
                    for c in range(16):
                        pt, Bp = pp.next()
                        for j in range(4):
                            P.op("pe", TR(pt[:, j * 128:(j + 1) * 128], xt[j][0][:, c * 128:(c + 1) * 128], ident_f[:]),
                                 reads=[xt[j][1], Bc], writes=[Bp], sig=(j == 3))
                        s_, Bs = sp_.next()
                        eng = "act" if c % 2 == 0 else "dve"
                        if eng == "act":
                            P.op("act", ACTF(s_[:], pt[:], AF.Identity), reads=[Bp], writes=[Bs])
                        else:
                            P.op("dve", CP(s_[:], pt[:]), reads=[Bp], writes=[Bs])
                        P.dma("sp", DMA(SC['hT'][c * 128:(c + 1) * 128, g * 512:(g + 1) * 512], s_[:]), Bs, reads=[Bs])
            P.new_phase()

        def phase1(l):
            with ExitStack() as st:
                def sb(name, shape, dt):
                    return SBT(name, shape, dt)

                def ps(name, shape, dt):
                    return PST(name, shape, dt)
                P.dma("sp", DMA(vt[:], vec[l]), Bvt, writes=[Bvt])
                hn = [st.enter_context(sb("hn%d" % c, [128, S], BF16)) for c in range(16)]
                Bhn = [[Buf("hn%d_%d" % (c, t)) for t in range(8)] for c in range(16)]
                hp = RPool(st, sb, "p1h", 16, [128, 512], F32)
                sqp = RPool(st, sb, "p1sq", 3, [128, 512], BF16)
                zbp = RPool(st, sb, "p1zb", 3, [128, 512], F32)
                rsp = RPool(st, sb, "p1rs", 2, [128, 512], F32)
                wp = RPool(st, sb, "p1w", 2, [128, 16, 256], BF16)
                obp = RPool(st, sb, "p1ob", 4, [128, 512], BF16)
                ofp = RPool(st, sb, "p1of", 2, [128, 512], F32)
                brp = RPool(st, sb, "p1br", 2, [128, 256], F32)
                pm = RPool(st, ps, "p1pm", 5, [128, 512], F32)
                p2 = RPool(st, ps, "p1p2", 2, [128, 512], F32)
                for t in range(8):
                    ts_ = slice(t * 512, (t + 1) * 512)
                    hts = []
                    for c in range(16):
                        h_, Bh = hp.next()
                        P.dma("sp", DMA(h_[:], SC['hT'][c * 128:(c + 1) * 128, ts_]), Bh, writes=[Bh])
                        hts.append((h_, Bh))
                    pst, Bpst = p2.next()
                    for c in range(16):
                        sq, Bsq = sqp.next()
                        P.op("act", ACTF(sq[:], hts[c][0][:], AF.Square), reads=[hts[c][1]], writes=[Bsq])
                        P.op("pe", MM(pst[:], om2048_b[:], sq[:], c == 0, c == 15), reads=[Bsq, Bc], writes=[Bpst])
                    rs, Brs = rsp.next()
                    P.op("act", ACTF(rs[:], pst[:], AF.Ln, bias=epsb[:, 0:1]), reads=[Bpst, Bc], writes=[Brs])
                    P.op("act", ACTF(rs[:], rs[:], AF.Exp, scale=-0.5), reads=[Brs], writes=[Brs])
                    for c in range(16):
                        eng = "dve"
                        P.op(eng, STT(hn[c][:, ts_], hts[c][0][:], vt[:, VCOL['ln_gain'] + c:VCOL['ln_gain'] + c + 1], rs[:], ALU.mult, ALU.mult),
                             reads=[hts[c][1], Bvt, Brs], writes=[Bhn[c][t]])
                pending = []

                def flush():
                    while pending:
                        pending.pop(0)()
                for name, kind, dst, o in P1_JOBS:
                    c0, n = OFF[name]
                    odt = F32 if o.get('odt') == 'f32' else BF16
                    for b0 in range(0, n, 256):
                        bn = min(256, n - b0)
                        wt, Bw = wp.next()
                        P.dma("pool", DMA(wt[:, :, :bn], w_in[l, :, c0 + b0:c0 + b0 + bn].rearrange("(c p) n -> p c n", p=128)), Bw, writes=[Bw])
                        if kind == 'tm':
                            br, Bbr = brp.next()
                            P.dma("sp", DMA(br[:, :bn], rows[l:l + 1, RCOL[name] + b0:RCOL[name] + b0 + bn].partition_broadcast(128)), Bbr, writes=[Bbr])
                            for tt in range(32):
                                pt, Bp = pm.next()
                                for c in range(16):
                                    P.op("pe", MM(pt[:, :bn], hn[c][:, tt * 128:(tt + 1) * 128], wt[:, c, :bn], c == 0, c == 15),
                                         reads=[Bw, Bhn[c][tt // 4]], writes=[Bp], sig=(c == 15))
                                flush()
                                if odt == F32:
                                    ob, Bo = ofp.next()
                                else:
                                    ob, Bo = obp.next()
                                if o.get('func') == 'sigmoid':
                                    zb, Bz = zbp.next()
                                    P.op("dve", TT(zb[:, :bn], pt[:, :bn], br[:, :bn], ALU.add), reads=[Bp, Bbr], writes=[Bz])
                                    P.op("act", ACTF(ob[:, :bn], zb[:, :bn], AF.Sigmoid), reads=[Bz], writes=[Bo])
                                else:
                                    P.op("dve", TT(ob[:, :bn], pt[:, :bn], br[:, :bn], ALU.add), reads=[Bp, Bbr], writes=[Bo])
                                P.dma("sp", DMA(SC[dst][tt * 128:(tt + 1) * 128, b0:b0 + bn], ob[:, :bn]), Bo, reads=[Bo])
                            continue
                        for m0 in range(0, bn, 128):
                            mm = min(128, bn - m0)
                            ch = (b0 + m0) // 128
                            bcol = VCOL['b_' + name] + ch
                            bias = vt[:mm, bcol:bcol + 1]
                            for t in range(8):
                                ts_ = slice(t * 512, (t + 1) * 512)
                                pt, Bp = pm.next()
                                for c in range(16):
                                    P.op("pe", MM(pt[:mm, :], wt[:, c, m0:m0 + mm], hn[c][:, ts_], c == 0, c == 15),
                                         reads=[Bw, Bhn[c][t]], writes=[Bp], sig=(c == 15))
                                flush()
                                drow = SC[dst][b0 + m0:b0 + m0 + mm, ts_]
                                if kind == 'fm':
                                    if odt == F32:
                                        ob, Bo = ofp.next()
                                    else:
                                        ob, Bo = obp.next()
                                    P.op("act", ACTF(ob[:mm, :], pt[:mm, :], FUNCS[o['func']], bias=bias), reads=[Bp, Bvt], writes=[Bo])
                                    P.dma("sp", DMA(drow, ob[:mm, :]), Bo, reads=[Bo])
                                else:
                                    zb, Bz = zbp.next()
                                    sq, Bsq = sqp.next()
                                    P.op("act", ACTF(zb[:], pt[:], AF.Identity, bias=bias), reads=[Bp, Bvt], writes=[Bz])
                                    P.op("act", ACTF(sq[:], pt[:], AF.Square, bias=bias), reads=[Bp, Bvt], writes=[Bsq])
                                    gcol = VCOL[o['gain']]
                                    qs = o.get('qscale', False)

                                    def post(zb=zb, Bz=Bz, sq=sq, Bsq=Bsq, gcol=gcol, qs=qs, drow=drow):
                                        pq, Bpq = p2.next()
                                        P.op("pe", MM(pq[:], (ones_b if qs else om128_b)[:], sq[:], True, True), reads=[Bsq, Bc], writes=[Bpq])
                                        rs, Brs = rsp.next()
                                        P.op("act", ACTF(rs[:], pq[:], AF.Ln, bias=epsb[:, (1 if qs else 0):(2 if qs else 1)]), reads=[Bpq, Bc], writes=[Brs])
                                        P.op("act", ACTF(rs[:], rs[:], AF.Exp, scale=-0.5), reads=[Brs], writes=[Brs])
                                        ob, Bo = obp.next()
                                        P.op("dve", STT(ob[:], zb[:], vt[:, gcol:gcol + 1], rs[:], ALU.mult, ALU.mult), reads=[Bz, Bvt, Brs], writes=[Bo])
                                        P.dma("sp", DMA(drow, ob[:]), Bo, reads=[Bo])
                                    pending.append(post)
                flush()
            P.new_phase()


        def phase2(l):
            H = S // 2
            with ExitStack() as st:
                def sb(name, shape, dt):
                    return st.enter_context(SBT(name, shape, dt))
                sets = []
                for k in range(2):
                    d = {}
                    for nm, shape, dt in (("up", [128, H + 3], F32), ("uc", [128, H], F32), ("ucb", [128, H], BF16), ("ra", [128, H], F32),
                                          ("ix", [128, H], F32), ("t1", [128, H], F32), ("hs", [128, H], F32), ("gt", [128, H], BF16), ("ya", [128, H], BF16)):
                        d[nm] = sb("l_%s%d" % (nm, k), shape, dt)
                        d["B" + nm] = Buf("%s%d" % (nm, k))
                    sets.append(d)
                wab = [sb("l_wa%d" % k, [128, 128], BF16) for k in range(2)]; Bwa = [Buf("wa%d" % k) for k in range(2)]
                wxb = [sb("l_wx%d" % k, [128, 128], BF16) for k in range(2)]; Bwx = [Buf("wx%d" % k) for k in range(2)]
                cv = sb("l_cv", [128, 8], F32); Bcv = Buf("cv")
                pp = RPool(st, PST, "l_ps", 4, [128, 512], F32)
                lc = VCOL['lru_lambda']
                P.op("act", ACTF(cv[:], vt[:, lc:lc + 8], AF.Exp, scale=-1.0), reads=[Bvt], writes=[Bcv])
                P.op("act", ACTF(cv[:], cv[:], AF.Ln, bias=one_f[:, 0:1]), reads=[Bcv, Bc], writes=[Bcv])
                P.op("dve", TS(cv[:], cv[:], -8.0, None, ALU.mult), reads=[Bcv], writes=[Bcv])
                def issue_loads(m):
                    if m >= 16:
                        return
                    c_, hf_ = m // 2, m % 2
                    d_ = sets[m % 2]
                    r_ = slice(c_ * 128, (c_ + 1) * 128)
                    if hf_ == 0:
                        P.op("dve", lambda e, up=d_["up"]: e.memset(up[:, 0:3], 0.0), writes=[d_["Bup"]])
                        P.dma("sp", DMA(d_["up"][:, 3:], SC['zlx'][r_, 0:H]), d_["Bup"], writes=[d_["Bup"]])
                    else:
                        P.dma("sp", DMA(d_["up"][:], SC['zlx'][r_, H - 3:S]), d_["Bup"], writes=[d_["Bup"]])
                    P.dma("sp", DMA(d_["gt"][:], SC['zlg'][r_, hf_ * H:hf_ * H + H]), d_["Bgt"], writes=[d_["Bgt"]])
                n = 0
                for c in range(8):
                    rows_ = slice(c * 128, (c + 1) * 128)
                    wk = c % 2
                    P.op("pool", lambda e, wk=wk: e.memset(wab[wk][:], 0.0), writes=[Bwa[wk]])
                    P.op("pool", lambda e, wk=wk: e.memset(wxb[wk][:], 0.0), writes=[Bwx[wk]])
                    for j in range(2):
                        P.dma("pool", DMA(wab[wk][j * 64:(j + 1) * 64, j * 64:(j + 1) * 64], lru_wa[l, 2 * c + j]), Bwa[wk], writes=[Bwa[wk]])
                        P.dma("pool", DMA(wxb[wk][j * 64:(j + 1) * 64, j * 64:(j + 1) * 64], lru_wx[l, 2 * c + j]), Bwx[wk], writes=[Bwx[wk]])
                    prev = None
                    for hf in range(2):
                        d = sets[n % 2]
                        n += 1
                        tb = hf * H
                        up, uc, ucb, ra, ix, t1, hsb, gt, ya = (d[k] for k in ("up", "uc", "ucb", "ra", "ix", "t1", "hs", "gt", "ya"))
                        Bup, Buc, Bucb, Bra, Bix, Bt1, Bhs, Bgt, Bya = (d["B" + k] for k in ("up", "uc", "ucb", "ra", "ix", "t1", "hs", "gt", "ya"))
                        if n == 1:
                            issue_loads(0)
                        issue_loads(n)
                        cw = VCOL['conv_w']
                        cb = VCOL['conv_b'] + c
                        P.op("dve", TS(uc[:], up[:, 0:H], vt[:, cw + c:cw + c + 1], vt[:, cb:cb + 1], ALU.mult, ALU.add), reads=[Bup, Bvt], writes=[Buc])
                        for j in range(1, 4):
                            P.op("dve", STT(uc[:], up[:, j:j + H], vt[:, cw + 8 * j + c:cw + 8 * j + c + 1], uc[:], ALU.mult, ALU.add), reads=[Bup, Bvt, Buc], writes=[Buc])
                        P.op("act", ACTF(ucb[:], uc[:], AF.Identity), reads=[Buc], writes=[Bucb])
                        ba = VCOL['lru_ba'] + c
                        bx = VCOL['lru_bx'] + c
                        for t in range(H // 512):
                            ts_ = slice(t * 512, (t + 1) * 512)
                            p1, Bp1 = pp.next()
                            P.op("pe", MM(p1[:], wab[wk][:], ucb[:, ts_], True, True), reads=[Bwa[wk], Bucb], writes=[Bp1])
                            P.op("act", ACTF(ra[:, ts_], p1[:], AF.Sigmoid, bias=vt[:, ba:ba + 1]), reads=[Bp1, Bvt], writes=[Bra])
                            p2_, Bp2 = pp.next()
                            P.op("pe", MM(p2_[:], wxb[wk][:], ucb[:, ts_], True, True), reads=[Bwx[wk], Bucb], writes=[Bp2])
                            P.op("act", ACTF(ix[:, ts_], p2_[:], AF.Sigmoid, bias=vt[:, bx:bx + 1]), reads=[Bp2, Bvt], writes=[Bix])
                        P.op("act", ACTF(ra[:], ra[:], AF.Exp, scale=cv[:, c:c + 1]), reads=[Bra, Bcv], writes=[Bra])
                        P.op("pool", TT(ix[:], ix[:], uc[:], ALU.mult), reads=[Bix, Buc], writes=[Bix])
                        P.op("act", ACTF(t1[:], ra[:], AF.Square), reads=[Bra], writes=[Bt1])
                        P.op("act", ACTF(t1[:], t1[:], AF.Ln, bias=one_f[:, 0:1], scale=-1.0), reads=[Bt1, Bc], writes=[Bt1])
                        P.op("act", ACTF(t1[:], t1[:], AF.Exp, scale=0.5), reads=[Bt1], writes=[Bt1])
                        P.op("dve", TT(ix[:], ix[:], t1[:], ALU.mult), reads=[Bix, Bt1], writes=[Bix])
                        if prev is None:
                            P.op("dve", lambda e, hsb=hsb, ra=ra, ix=ix: e.tensor_tensor_scan(out=hsb[:], data0=ra[:], data1=ix[:], initial=0.0, op0=ALU.mult, op1=ALU.add),
                                 reads=[Bra, Bix], writes=[Bhs])
                        else:
                            ph, Bph = prev
                            P.op("dve", lambda e, hsb=hsb, ra=ra, ix=ix, ph=ph: e.tensor_tensor_scan(out=hsb[:], data0=ra[:], data1=ix[:], initial=ph[:, H - 1:H], op0=ALU.mult, op1=ALU.add),
                                 reads=[Bra, Bix, Bph], writes=[Bhs])
                        prev = (hsb, Bhs)
                        P.op("pool", TT(ya[:], hsb[:], gt[:], ALU.mult), reads=[Bhs, Bgt], writes=[Bya])
                        P.dma("sp", DMA(SC['yaT'][rows_, tb:tb + H], ya[:]), Bya, reads=[Bya])
                        if 'hs' in dbg:
                            P.dma("sp", DMA(SC['hs'][rows_, tb:tb + H], hsb[:]), Bhs, reads=[Bhs])
            P.new_phase()

        def phase3(l):
            with ExitStack() as st:
                def sb(name, shape, dt):
                    return st.enter_context(SBT(name, shape, dt))

                def ps(name, shape, dt):
                    return PST(name, shape, dt)
                w1 = [sb("c_w1_%d" % kv, [128, 32, 256], BF16) for kv in range(2)]
                Bw1 = [Buf("w1_%d" % kv) for kv in range(2)]
                w2 = sb("c_w2", [128, 2, 2, 128], BF16); Bw2 = Buf("w2")
                xg = [sb("c_xg%d" % i, [128, S], BF16) for i in range(4)]
                Bxg = [Buf("xg%d" % i) for i in range(4)]
                posb = sb("c_pos", [128, 64], BF16); Bpos = Buf("pos")
                pbs = sb("c_pb", [128, 4], F32); Bpb = Buf("pb")
                hid = RPool(st, SBT, "c_hid", 4, [128, 256], BF16)
                sqs = RPool(st, SBT, "c_sq", 2, [128, 256], BF16)
                zbs = RPool(st, SBT, "c_zb", 2, [128, 256], F32)
                rss = RPool(st, SBT, "c_rs", 2, [128, 256], F32)
                pp = RPool(st, ps, "c_ps", 6, [128, 512], F32)
                for kv in range(2):
                    P.dma("pool", DMA(w1[kv][:], cmp_w1[l, kv].rearrange("(i d) h -> d i h", d=128)), Bw1[kv], writes=[Bw1[kv]])
                    P.dma("pool", DMA(w2[:, kv], cmp_w2[l, kv].rearrange("(c p) d -> p c d", p=128)), Bw2, writes=[Bw2])
                    src = SC['kci'] if kv == 0 else SC['vci']
                    for g in range(2):
                        P.dma("sp", DMA(xg[kv * 2 + g][:], src[g * 128:(g + 1) * 128, :]), Bxg[kv * 2 + g], writes=[Bxg[kv * 2 + g]])
                pc = VCOL['cmp_pos']
                P.op("dve", CP(posb[:], vt[:, pc:pc + 64]), reads=[Bvt], writes=[Bpos])
                P.op("dve", lambda e: e.memset(kcT[:], 0.0), writes=[Bkc])
                P.op("pool", lambda e: e.memset(vco[:], 0.0), writes=[Bvco])
                for cc in range(2):
                    for g in range(2):
                        P.dma("pool", DMA(vco[:, cc, g, 128:193], c_ovl[cc * 128:(cc + 1) * 128, :]), Bvco, writes=[Bvco])
                for kv in range(2):
                    for hc in range(2):
                        pt, Bp = pp.next()
                        for i in range(32):
                            P.op("pe", MM(pt[:, 0:1], w1[kv][:, i, hc * 128:(hc + 1) * 128], posb[:, kv * 32 + i:kv * 32 + i + 1], i == 0, i == 31),
                                 reads=[Bw1[kv], Bpos], writes=[Bp], sig=(i == 31))
                        P.op("dve", CP(pbs[:, kv * 2 + hc:kv * 2 + hc + 1], pt[:, 0:1]), reads=[Bp], writes=[Bpb])
                for kv in range(2):
                    for g in range(2):
                        x_ = xg[kv * 2 + g]
                        hts = []
                        for hc in range(2):
                            pt, Bp = pp.next()
                            for i in range(32):
                                P.op("pe", MM(pt[:, 0:255], w1[kv][:, i, hc * 128:(hc + 1) * 128], x_[:, i:i + 16 * 254 + 1:16], i == 0, i == 31),
                                     reads=[Bw1[kv], Bxg[kv * 2 + g]], writes=[Bp], sig=(i == 31))
                            h_, Bh = hid.next()
                            P.op("act", ACTF(h_[:, 0:255], pt[:, 0:255], AF.Silu, bias=pbs[:, kv * 2 + hc:kv * 2 + hc + 1]), reads=[Bp, Bpb], writes=[Bh])
                            hts.append((h_, Bh))
                        if kv == 0:
                            pt, Bp = pp.next()
                            for hc in range(2):
                                P.op("pe", MM(pt[:, 0:255], w2[:, 0, hc, :], hts[hc][0][:, 0:255], hc == 0, hc == 1),
                                     reads=[Bw2, hts[hc][1]], writes=[Bp], sig=(hc == 1))
                            zb, Bz = zbs.next()
                            sq, Bsq = sqs.next()
                            P.op("act", ACTF(zb[:, 0:255], pt[:, 0:255], AF.Identity), reads=[Bp], writes=[Bz])
                            P.op("act", ACTF(sq[:, 0:255], pt[:, 0:255], AF.Square), reads=[Bp], writes=[Bsq])
                            pq, Bpq = pp.next()
                            P.op("pe", MM(pq[:, 0:255], om128_b[:], sq[:, 0:255], True, True), reads=[Bsq, Bc], writes=[Bpq])
                            rs, Brs = rss.next()
                            P.op("act", ACTF(rs[:, 0:255], pq[:, 0:255], AF.Ln, bias=epsb[:, 0:1]), reads=[Bpq, Bc], writes=[Brs])
                            P.op("act", ACTF(rs[:, 0:255], rs[:, 0:255], AF.Exp, scale=-0.5), reads=[Brs], writes=[Brs])
                            gk = VCOL['nsa_k_gain']
                            P.op("dve", STT(kcT[:, g, 0:255], zb[:, 0:255], vt[:, gk:gk + 1], rs[:, 0:255], ALU.mult, ALU.mult),
                                 reads=[Bz, Bvt, Brs], writes=[Bkc])
                        else:
                            for cc in range(2):
                                n = 128 if cc == 0 else 127
                                pt, Bp = pp.next()
                                for hc in range(2):
                                    P.op("pe", MM(pt[:n, 0:128], hts[hc][0][:, cc * 128:cc * 128 + n], w2[:, 1, hc, :], hc == 0, hc == 1),
                                         reads=[Bw2, hts[hc][1]], writes=[Bp], sig=(hc == 1))
                                P.op("dve", CP(vco[:n, cc, g, 0:128], pt[:n, 0:128]), reads=[Bp], writes=[Bvco])
                if 'kcd' in dbg:
                    P.dma("sp", DMA(SC['kcd'], kcT[:]), Bkc, reads=[Bkc])
                    P.dma("sp", DMA(SC['vcd'], vco[:]), Bvco, reads=[Bvco])
            P.new_phase()


        def phase4(l, qts=range(32)):
            with ExitStack() as st:
                def sb(name, shape, dt):
                    return st.enter_context(SBT(name, shape, dt))
                NEG = 30000.0
                ksT = sb("n_ks", [128, 2, S], BF16); Bks = Buf("ks")
                kwT = sb("n_kw", [128, 2, S], BF16); Bkw = Buf("kw")
                vs = sb("n_vs", [128, 32, 2, 130], BF16); Bvs = Buf("vs")
                vw = sb("n_vw", [128, 32, 2, 130], BF16); Bvw = Buf("vw")
                bgt = sb("n_bg", [128, 32, 24], F32); Bbg = Buf("bg")
                cmk = sb("n_cmk", [128, 17, 128], BF16); Bcmk = Buf("cmk")
                tkm = sb("n_tkm", [128, 32, 64], F32); Btkm = Buf("tkm")
                tka = sb("n_tka", [128, 32, 64], F32); Btka = Buf("tka")
                esl = sb("n_esl", [64, 32, 128], BF16); Besl = Buf("esl")
                qp = RPool(st, SBT, "n_q", 3, [128, 8, 128], BF16)
                gp = RPool(st, SBT, "n_g", 3, [128, 8, 128], BF16)
                ptp = RPool(st, SBT, "n_pt", 6, [128, 512], BF16)
                yp = RPool(st, SBT, "n_y", 2, [128, 4, 128], F32)
                ybp = RPool(st, SBT, "n_yb", 2, [128, 4, 128], BF16)
                yop = RPool(st, SBT, "n_yo", 3, [128, 4, 128], BF16)
                smp = RPool(st, SBT, "n_sm", 2, [128, 64 * 4 + 32], F32)
                sbp = RPool(st, SBT, "n_sb", 2, [128, 64], BF16)
                s4p = RPool(st, SBT, "n_s4", 2, [64, 4, 128], BF16)
                stp = RPool(st, PST, "n_st", 3, [128, 512], F32)
                acp = RPool(st, PST, "n_ac", 2, [128, 4, 256], F32)
                mip = RPool(st, PST, "n_mi", 1, [128, 512], BF16)
                for g in range(2):
                    P.dma("sp", DMA(ksT[:, g, :], SC['ksl'][g * 128:(g + 1) * 128, :]), Bks, writes=[Bks])
                    P.dma("sp", DMA(kwT[:, g, :], SC['kwn'][g * 128:(g + 1) * 128, :]), Bkw, writes=[Bkw])
                    P.dma("sp", DMA(vs[:, :, g, 0:128], SC['vsl'][:, g * 128:(g + 1) * 128].rearrange("(k p) d -> p k d", p=128)), Bvs, writes=[Bvs])
                    P.dma("sp", DMA(vw[:, :, g, 0:128], SC['vwn'][:, g * 128:(g + 1) * 128].rearrange("(k p) d -> p k d", p=128)), Bvw, writes=[Bvw])
                P.op("dve", lambda e: e.memset(vs[:, :, :, 128:129], 1.0), writes=[Bvs])
                P.op("dve", lambda e: e.memset(vw[:, :, :, 128:129], 1.0), writes=[Bvw])
                P.dma("sp", DMA(bgt[:], SC['bg'].rearrange("(k p) c -> p k c", p=128)), Bbg, writes=[Bbg])
                P.dma("pool", DMA(cmk[:], c_cmask), Bcmk, writes=[Bcmk])
                P.dma("sp", DMA(tkm[:], c_tkmul), Btkm, writes=[Btkm])
                P.dma("sp", DMA(tka[:], c_tkadd), Btka, writes=[Btka])
                P.dma("pool", DMA(esl[:], c_esel), Besl, writes=[Besl])
                qn_v = SC['qn'].rearrange("(h d) t -> d h t", d=128)
                ng_v = SC['ng'].rearrange("(h d) t -> d h t", d=128)
                yb_v = SC['ybT'].rearrange("(h d) t -> d h t", d=128)

                def attend(acc, Bacc, keysT, Bk, vals, Bv, g, q_, Bq, kcs, qt, bias=None, vw_=129, pre=None):
                    kcs = list(kcs)
                    LA = 2
                    pend = {}

                    def stage1(n_):
                        kc = kcs[n_]
                        s_, Bs = stp.next()
                        rhs = q_[:, 4 * g:4 * g + 4, :]
                        P.op("pe", MM(s_[:], keysT[:, g, kc * 128:(kc + 1) * 128], rhs, True, bias is None), reads=[Bk, Bq], writes=[Bs], sig=(bias is None))
                        if bias is not None:
                            P.op("pe", MM(s_[:], esl[:, kc, :], bias[0][:].rearrange("p a b -> p (a b)"), False, True), reads=[Besl, bias[1]], writes=[Bs])
                        p_, Bp = ptp.next()
                        P.op("act", ACTF(p_[:], s_[:], AF.Exp), reads=[Bs], writes=[Bp])
                        m = pre(kc) if pre is not None else None
                        if m is not None:
                            pv = p_[:].rearrange("p (a b) -> p a b", a=4)
                            P.op("pool", TT(pv, pv, m[0].unsqueeze(1).to_broadcast([128, 4, 128]), ALU.mult), reads=[Bp, m[1]], writes=[Bp])
                        return p_, Bp

                    def stage2(n_, p_, Bp):
                        kc = kcs[n_]
                        for hh in range(4):
                            P.op("pe", MM(acc[:, hh, 0:vw_], p_[:, hh * 128:(hh + 1) * 128], vals(kc), n_ == 0 and hh % 2 == 0, n_ == len(kcs) - 1, skip=True),
                                 reads=[Bp, Bv], writes=[Bacc], sig=(hh == 3))
                    for n_ in range(len(kcs) + LA):
                        if n_ < len(kcs):
                            pend[n_] = stage1(n_)
                        if n_ - LA >= 0:
                            stage2(n_ - LA, *pend.pop(n_ - LA))

                deferred = []
                for qt in qts:
                    q0 = qt * 128
                    q_, Bq = qp.next()
                    P.dma("sp", DMA(q_[:], qn_v[:, :, q0:q0 + 128]), Bq, writes=[Bq])
                    gt_, Bg = gp.next()
                    P.dma("sp", DMA(gt_[:], ng_v[:, :, q0:q0 + 128]), Bg, writes=[Bg])
                    for g in range(2):
                        sm, Bsm = smp.next()
                        imp = sm[:, 0:64]
                        sco = sm[:, 64:128]
                        wrk = sm[:, 128:192]
                        selm = sm[:, 192:256]
                        mx8 = sm[:, 256:264]
                        rd = sm[:, 264:268]
                        cf = sm[:, 268:272]
                        rd2 = sm[:, 272:276]
                        cf2 = sm[:, 276:280]
                        bgv = bgt[:, qt, :].rearrange("p (h c) -> p h c", c=3)
                        y_, By = yp.next()
                        accA, BaA = acp.next()
                        ncc = 1 if qt < 16 else 2

                        def preA(cc, qt=qt):
                            di = qt - 16 * cc
                            return (cmk[:, di, :], Bcmk) if di <= 16 else None
                        attend(accA, BaA, kcT, Bkc, lambda cc, g=g: vco[:, cc, g, :], Bvco, g, q_, Bq, range(ncc), qt, vw_=193, pre=preA)
                        P.op("dve", TS(rd, accA[:, :, 192], 1e-30, None, ALU.max), reads=[BaA], writes=[Bsm])
                        P.op("dve", lambda e, rd=rd: e.reciprocal(out=rd, in_=rd), reads=[Bsm], writes=[Bsm])
                        P.op("dve", TS(imp, accA[:, 0, 128:192], rd[:, 0:1], None, ALU.mult), reads=[BaA, Bsm], writes=[Bsm])
                        for hh in range(1, 4):
                            P.op("dve", STT(imp, accA[:, hh, 128:192], rd[:, hh:hh + 1], imp, ALU.mult, ALU.add), reads=[BaA, Bsm], writes=[Bsm])
                        P.op("dve", TT(sco, imp, tkm[:, qt, :], ALU.mult), reads=[Bsm, Btkm], writes=[Bsm])
                        P.op("dve", TT(sco, sco, tka[:, qt, :], ALU.add), reads=[Bsm, Btka], writes=[Bsm])
                        P.op("dve", lambda e, mx8=mx8, sco=sco: e.max(out=mx8, in_=sco), reads=[Bsm], writes=[Bsm])
                        P.op("dve", lambda e, wrk=wrk, mx8=mx8, sco=sco: e.match_replace(out=wrk, in_to_replace=mx8, in_values=sco, imm_value=-3e6), reads=[Bsm], writes=[Bsm])
                        P.op("dve", lambda e, mx8=mx8, wrk=wrk: e.max(out=mx8, in_=wrk), reads=[Bsm], writes=[Bsm])
                        P.op("dve", lambda e, wrk=wrk, mx8=mx8: e.match_replace(out=wrk, in_to_replace=mx8, in_values=wrk, imm_value=-3e6), reads=[Bsm], writes=[Bsm])
                        P.op("dve", TT(selm, sco, wrk, ALU.subtract), reads=[Bsm], writes=[Bsm])
                        P.op("dve", TS(selm, selm, 1.0, None, ALU.min), reads=[Bsm], writes=[Bsm])
                        sb_, Bsb = sbp.next()
                        P.op("dve", TS(sb_[:], selm, -1.0, NEG, ALU.add, ALU.mult), reads=[Bsm], writes=[Bsb])
                        P.op("dve", TT(cf, bgv[:, 4 * g:4 * g + 4, 0], rd, ALU.mult), reads=[Bbg, Bsm], writes=[Bsm])
                        for hh in range(4):
                            P.op("dve", TS(y_[:, hh, :], accA[:, hh, 0:128], cf[:, hh:hh + 1], None, ALU.mult), reads=[BaA, Bsm], writes=[By])
                        accC, BaC = acp.next()

                        def preC(kc, qt=qt):
                            if kc == qt:
                                return (tri_b[:], Bc)
                            if kc == qt - 4:
                                return (triu_b[:], Bc)
                            return None
                        attend(accC, BaC, kwT, Bkw, lambda kc, g=g: vw[:, kc, g, 0:129], Bvw, g, q_, Bq, range(max(0, qt - 4), qt + 1), qt, pre=preC)
                        while deferred:
                            deferred.pop(0)()
                        mi, Bmi = mip.next()
                        P.op("pe", TR(mi[0:64, 0:128], sb_[:], ident_b[:]), reads=[Bsb, Bc], writes=[Bmi])
                        s4, Bs4 = s4p.next()
                        P.op("dve", CP(s4[:], mi[0:64, 0:128].unsqueeze(1).to_broadcast([64, 4, 128])), reads=[Bmi], writes=[Bs4])
                        P.op("dve", lambda e, rd2=rd2, accC=accC: e.reciprocal(out=rd2, in_=accC[:, :, 128]), reads=[BaC], writes=[Bsm])
                        P.op("dve", TT(cf2, bgv[:, 4 * g:4 * g + 4, 2], rd2, ALU.mult), reads=[Bbg, Bsm], writes=[Bsm])
                        for hh in range(4):
                            P.op("dve", STT(y_[:, hh, :], accC[:, hh, 0:128], cf2[:, hh:hh + 1], y_[:, hh, :], ALU.mult, ALU.add), reads=[BaC, Bsm, By], writes=[By])
                        accB, BaB = acp.next()
                        attend(accB, BaB, ksT, Bks, lambda kc, g=g: vs[:, kc, g, 0:129], Bvs, g, q_, Bq, range(qt + 1), qt, bias=(s4, Bs4),
                               pre=lambda kc, qt=qt: (tri_b[:], Bc) if kc == qt else None)
                        P.op("dve", lambda e, rd=rd, accB=accB: e.reciprocal(out=rd, in_=accB[:, :, 128]), reads=[BaB], writes=[Bsm])
                        P.op("dve", TT(cf, bgv[:, 4 * g:4 * g + 4, 1], rd, ALU.mult), reads=[Bbg, Bsm], writes=[Bsm])
                        yb_, Byb = ybp.next()
                        for hh in range(4):
                            P.op("dve", STT(yb_[:, hh, :], accB[:, hh, 0:128], cf[:, hh:hh + 1], y_[:, hh, :], ALU.mult, ALU.add), reads=[BaB, Bsm, By], writes=[Byb])

                        def emitD(yb_=yb_, Byb=Byb, gt_=gt_, Bg=Bg, g=g, q0=q0):
                            mi, Bmi = mip.next()
                            for hh in range(4):
                                P.op("pe", TR(mi[:, hh * 128:(hh + 1) * 128], yb_[:, hh, :], ident_b[:]), reads=[Byb, Bc], writes=[Bmi], sig=(hh == 3))
                            yo, Byo = yop.next()
                            P.op("dve", TT(yo[:], mi[:].rearrange("p (a b) -> p a b", a=4), gt_[:, 4 * g:4 * g + 4, :], ALU.mult), reads=[Bmi, Bg], writes=[Byo])
                            P.dma("sp", DMA(yb_v[:, 4 * g:4 * g + 4, q0:q0 + 128], yo[:]), Byo, reads=[Byo])
                        deferred.append(emitD)
                while deferred:
                    deferred.pop(0)()
            P.new_phase()

        def phase5(l, heads=range(8), qss=range(8)):
            with ExitStack() as st:
                def sb(name, shape, dt):
                    return st.enter_context(SBT(name, shape, dt))
                fft = sb("f_ff", [8, S], F32); Bff = Buf("ff")
                onesr = sb("f_ones", [8, S], F32); Bon = Buf("ones")
                cum = sb("f_cum", [8, S], F32); Bcum = Buf("cum")
                dd = sb("f_dd", [8, 8, 8], F32); Bdd = Buf("dd")
                ones8 = sb("f_ones8", [8, 128], F32)
                ctok = sb("f_ctok", [128, 32, 8], F32); Bct = Buf("ctok")
                crefb = sb("f_cref", [128, 8, 8], F32); Bcr = Buf("cref")
                qh = RPool(st, SBT, "f_q", 2, [128, S], BF16)
                kh = RPool(st, SBT, "f_k", 2, [128, S], BF16)
                vh = RPool(st, SBT, "f_v", 2, [128, 32, 130], BF16)
                gp = RPool(st, SBT, "f_g", 3, [128, 512], BF16)
                btp = RPool(st, SBT, "f_bt", 3, [128, 32], F32)
                ptp = RPool(st, SBT, "f_pt", 8, [128, 512], BF16)
                rdp = RPool(st, SBT, "f_rd", 3, [128, 512], F32)
                ybp = RPool(st, SBT, "f_yb", 2, [128, 4, 128], BF16)
                yop = RPool(st, SBT, "f_yo", 3, [128, 512], BF16)
                stp = RPool(st, PST, "f_st", 4, [128, 512], F32)
                acp = RPool(st, PST, "f_ac", 2, [128, 2, 512], F32)
                P.dma("sp", DMA(fft[:], SC['ff']), Bff, writes=[Bff])
                P.op("dve", lambda e: e.memset(onesr[:], 1.0), writes=[Bon])
                P.op("dve", lambda e: e.memset(ones8[:], 1.0), writes=[Bon])
                P.op("act", ACTF(fft[:], fft[:], AF.Exp, scale=-1.0), reads=[Bff], writes=[Bff])
                P.op("act", ACTF(fft[:], fft[:], AF.Ln, bias=one_f[0:8, 0:1]), reads=[Bff, Bc], writes=[Bff])
                P.op("dve", lambda e: e.tensor_tensor_scan(out=cum[:], data0=onesr[:], data1=fft[:], initial=0.0, op0=ALU.mult, op1=ALU.add),
                     reads=[Bon, Bff], writes=[Bcum])
                if 'cf' in dbg:
                    P.dma("sp", DMA(SC['cf'], cum[:]), Bcum, reads=[Bcum])
                mt, Bmt = stp.next()
                for kc in range(32):
                    P.op("pe", TR(mt[:, kc * 8:(kc + 1) * 8], cum[:, kc * 128:(kc + 1) * 128], ident_f[0:8, 0:8]), reads=[Bcum, Bc], writes=[Bmt], sig=(kc == 31))
                P.op("dve", CP(ctok[:].rearrange("p k h -> p (k h)"), mt[:, 0:256]), reads=[Bmt], writes=[Bct])
                P.op("dve", lambda e: e.memset(dd[:], 0.0), writes=[Bdd])
                P.op("dve", TT(dd[:, :, 1:8], ident_f[0:8, 0:8].unsqueeze(2).to_broadcast([8, 8, 7]),
                               cum[:, 511:511 + 512 * 6 + 1:512].unsqueeze(1).to_broadcast([8, 8, 7]), ALU.mult), reads=[Bc, Bcum], writes=[Bdd])
                mt2, Bmt2 = stp.next()
                P.op("pe", MM(mt2[:, 0:64], ones8[:], dd[:].rearrange("p a b -> p (a b)"), True, True), reads=[Bon, Bdd], writes=[Bmt2])
                P.op("dve", CP(crefb[:].rearrange("p a b -> p (a b)"), mt2[:, 0:64]), reads=[Bmt2], writes=[Bcr])
                for h in heads:
                    q_, Bq = qh.next()
                    k_, Bk = kh.next()
                    v_, Bv = vh.next()
                    P.dma("sp", DMA(q_[:], SC['fq'][h * 128:(h + 1) * 128, :]), Bq, writes=[Bq])
                    P.dma("sp", DMA(k_[:], SC['fk'][h * 128:(h + 1) * 128, :]), Bk, writes=[Bk])
                    P.dma("sp", DMA(v_[:, :, 0:128], SC['fv'][:, h * 128:(h + 1) * 128].rearrange("(k p) d -> p k d", p=128)), Bv, writes=[Bv])
                    P.op("pool", lambda e, v_=v_: e.memset(v_[:, :, 128:129], 1.0), writes=[Bv])
                    for qs in qss:
                        q0 = qs * 512
                        nk = 4 * qs + 4
                        bt, Bbt = btp.next()
                        P.op("dve", TS(bt[:, 0:nk], ctok[:, 0:nk, h], crefb[:, h, qs:qs + 1], None, ALU.subtract), reads=[Bct, Bcr], writes=[Bbt])
                        g_, Bg = gp.next()
                        P.dma("sp", DMA(g_[:], SC['fg'][h * 128:(h + 1) * 128, q0:q0 + 512]), Bg, writes=[Bg])
                        acc, Bacc = acp.next()
                        LA = 3
                        pend = {}

                        def stage1(kc, qs=qs, q0=q0, q_=q_, Bq=Bq, k_=k_, Bk=Bk, bt=bt, Bbt=Bbt):
                            j = kc - 4 * qs
                            c0 = max(0, j) * 128
                            s_, Bs = stp.next()
                            P.op("pe", MM(s_[:, c0:512], k_[:, kc * 128:(kc + 1) * 128], q_[:, q0 + c0:q0 + 512], True, True), reads=[Bk, Bq], writes=[Bs])
                            p_, Bp = ptp.next()
                            P.op("act", ACTF(p_[:, c0:512], s_[:, c0:512], AF.Exp, bias=bt[:, kc:kc + 1]), reads=[Bs, Bbt], writes=[Bp])
                            if j >= 0:
                                P.op("pool", TT(p_[:, c0:c0 + 128], p_[:, c0:c0 + 128], tri_b[:], ALU.mult), reads=[Bp, Bc], writes=[Bp])
                            return p_, Bp

                        def stage2(kc, p_, Bp, qs=qs, acc=acc, Bacc=Bacc, v_=v_, Bv=Bv, nk=nk):
                            c0 = max(0, kc - 4 * qs) * 128
                            P.op("pe", MM(acc[:, 0, c0:512], v_[:, kc, 0:128], p_[:, c0:512], kc == 0, kc == nk - 1), reads=[Bp, Bv], writes=[Bacc], sig=False)
                            P.op("pe", MM(acc[:, 1, c0:512], ones_b[:], p_[:, c0:512], kc == 0, kc == nk - 1), reads=[Bp, Bc], writes=[Bacc])
                        for kc in range(nk + LA):
                            if kc < nk:
                                pend[kc] = stage1(kc)
                            if kc - LA >= 0:
                                stage2(kc - LA, *pend.pop(kc - LA))
                        rd, Brd = rdp.next()
                        P.op("dve", lambda e, rd=rd, acc=acc: e.reciprocal(out=rd[:], in_=acc[:, 1, :]), reads=[Bacc], writes=[Brd])
                        P.op("dve", TT(rd[:], rd[:], acc[:, 0, :], ALU.mult), reads=[Bacc, Brd], writes=[Brd])
                        yo, Byo = yop.next()
                        P.op("pool", TT(yo[:], rd[:], g_[:], ALU.mult), reads=[Brd, Bg], writes=[Byo])
                        P.dma("sp", DMA(SC['ycT'][h * 128:(h + 1) * 128, q0:q0 + 512], yo[:]), Byo, reads=[Byo])
            P.new_phase()


        def phase6(l, halves=range(2)):
            for hf in halves:
                t0 = hf * 2048
                with ExitStack() as st:
                    def sb(name, shape, dt):
                        return st.enter_context(SBT(name, shape, dt))
                    yy = [sb("m_y%d" % i, [128, 8, 2048], BF16) for i in range(3)]
                    Byy = [[Buf("y%d_%d" % (i, t)) for t in range(4)] for i in range(3)]
                    wbp = RPool(st, SBT, "m_wb", 9, [128, 8, 128], BF16)
                    mgp = RPool(st, SBT, "m_mg", 6, [128, 2048], BF16)
                    m1p = RPool(st, SBT, "m_m1", 6, [128, 512], F32)
                    mop = RPool(st, SBT, "m_mo", 3, [128, 512], BF16)
                    pp = RPool(st, PST, "m_ps", 6, [128, 512], F32)
                    for t in range(4):
                        for i, nm in enumerate(('yaT', 'ybT', 'ycT')):
                            P.dma("sp", DMA(yy[i][:, :, t * 512:(t + 1) * 512], SC[nm][:, t0 + t * 512:t0 + (t + 1) * 512].rearrange("(c p) t -> p c t", p=128)),
                                  Byy[i][t], writes=[Byy[i][t]])
                    for f in range(16):
                        ws, ms = [], []
                        for br in range(3):
                            w_, Bw = wbp.next()
                            P.dma("pool", DMA(w_[:], w_branch[l, br, f]), Bw, writes=[Bw])
                            ws.append((w_, Bw))
                            m_, Bm = mgp.next()
                            r0 = br * 2048 + f * 128
                            P.dma("sp", DMA(m_[:], SC['mg'][r0:r0 + 128, t0:t0 + 2048]), Bm, writes=[Bm])
                            ms.append((m_, Bm))
                        for t in range(4):
                            ts_ = slice(t * 512, (t + 1) * 512)
                            parts = []
                            for br in range(3):
                                pt, Bp = pp.next()
                                for c in range(8):
                                    P.op("pe", MM(pt[:], ws[br][0][:, c, :], yy[br][:, c, ts_], c == 0, c == 7), reads=[ws[br][1], Byy[br][t]], writes=[Bp], sig=(c == 7))
                                m1, Bm1 = m1p.next()
                                P.op("dve", TT(m1[:], pt[:], ms[br][0][:, ts_], ALU.mult), reads=[Bp, ms[br][1]], writes=[Bm1])
                                parts.append((m1, Bm1))
                            P.op("dve", TT(parts[0][0][:], parts[0][0][:], parts[1][0][:], ALU.add), reads=[parts[0][1], parts[1][1]], writes=[parts[0][1]])
                            mo, Bmo = mop.next()
                            P.op("dve", TT(mo[:], parts[0][0][:], parts[2][0][:], ALU.add), reads=[parts[0][1], parts[2][1]], writes=[Bmo])
                            P.dma("sp", DMA(SC['mgd'][f * 128:(f + 1) * 128, t0 + t * 512:t0 + (t + 1) * 512], mo[:]), Bmo, reads=[Bmo])
                P.new_phase()
                with ExitStack() as st2:
                    hmb = st2.enter_context(SBT("m_hmb", [128, 16, 2048], BF16))
                    Bhm = [[Buf("hmb%d_%d" % (f, t)) for t in range(4)] for f in range(16)]
                    with ExitStack() as st:
                        def sb(name, shape, dt):
                            return st.enter_context(SBT(name, shape, dt))
                        mgd = sb("m_mgd", [128, 16, 2048], BF16); Bmgd = [Buf("mgd%d" % t) for t in range(4)]
                        wop = RPool(st, SBT, "m_wo", 3, [128, 16, 128], BF16)
                        hip = RPool(st, SBT, "m_hi", 3, [128, 512], F32)
                        hop = RPool(st, SBT, "m_ho", 3, [128, 512], F32)
                        pp = RPool(st, PST, "m_pb", 4, [128, 512], F32)
                        for t in range(4):
                            P.dma("sp", DMA(mgd[:, :, t * 512:(t + 1) * 512], SC['mgd'][:, t0 + t * 512:t0 + (t + 1) * 512].rearrange("(c p) t -> p c t", p=128)),
                                  Bmgd[t], writes=[Bmgd[t]])
                        for f in range(16):
                            w_, Bw = wop.next()
                            P.dma("pool", DMA(w_[:], w_out[l, f]), Bw, writes=[Bw])
                            for t in range(4):
                                ts_ = slice(t * 512, (t + 1) * 512)
                                hsl = SC['hT'][f * 128:(f + 1) * 128, t0 + t * 512:t0 + (t + 1) * 512]
                                hi, Bhi = hip.next()
                                P.dma("sp", DMA(hi[:], hsl), Bhi, writes=[Bhi])
                                pt, Bp = pp.next()
                                for c in range(16):
                                    P.op("pe", MM(pt[:], w_[:, c, :], mgd[:, c, ts_], c == 0, c == 15), reads=[Bw, Bmgd[t]], writes=[Bp], sig=(c == 15))
                                ho, Bho = hop.next()
                                P.op("dve", TT(ho[:], pt[:], hi[:], ALU.add), reads=[Bp, Bhi], writes=[Bho])
                                P.op("act", ACTF(hmb[:, f, ts_], ho[:], AF.Identity), reads=[Bho], writes=[Bhm[f][t]])
                                P.dma("sp", DMA(hsl, ho[:]), Bho, reads=[Bho])
                    P.new_phase()
                    with ExitStack() as st:
                        def sb(name, shape, dt):
                            return st.enter_context(SBT(name, shape, dt))
                        wpl = sb("e_wpl", [128, 2, D], BF16); Bwpl = Buf("wpl")
                        pT = sb("e_pT", [128, 2, 2048], BF16); BpT = [Buf("pT%d" % i) for i in range(16)]
                        eall = sb("e_all", [128, 16, 2048], BF16); Bea = [Buf("ea%d" % i) for i in range(8)]
                        epre = sb("e_pre", [128, 16, 256], F32); Bep = Buf("epre")
                        pip = RPool(st, SBT, "e_pi", 3, [128, 256], F32)
                        sqp = RPool(st, SBT, "e_sq", 4, [128, 256], BF16)
                        rsp = RPool(st, SBT, "e_rs", 2, [128, 256], F32)
                        wgp = RPool(st, SBT, "e_wg", 3, [128, 16, 128], BF16)
                        hip = RPool(st, SBT, "e_hi", 2, [128, 512], F32)
                        sgp = RPool(st, SBT, "e_sg", 2, [128, 512], F32)
                        hop = RPool(st, SBT, "e_ho", 2, [128, 512], F32)
                        pp = RPool(st, PST, "e_ps", 4, [128, 512], F32)
                        p2 = RPool(st, PST, "e_p2", 2, [128, 512], F32)
                        P.dma("pool", DMA(wpl[:], w_ple[l].rearrange("(c p) n -> p c n", p=128)), Bwpl, writes=[Bwpl])
                        for tt in range(16):
                            pi, Bpi = pip.next()
                            r0 = t0 + tt * 128
                            P.dma("sp", DMA(pi[:], p_in[l, r0:r0 + 128, :]), Bpi, writes=[Bpi])
                            pt, Bp = pp.next()
                            for c in range(2):
                                P.op("pe", TR(pt[:, c * 128:(c + 1) * 128], pi[:, c * 128:(c + 1) * 128], ident_f[:]), reads=[Bpi, Bc], writes=[Bp], sig=(c == 1))
                            P.op("dve", CP(pT[:, :, tt * 128:(tt + 1) * 128], pt[:, 0:256].rearrange("p (c t) -> p c t", c=2)), reads=[Bp], writes=[BpT[tt]])
                        pg = VCOL['ple_gain']
                        for t in range(8):
                            ts_ = slice(t * 256, (t + 1) * 256)
                            pq, Bpq = p2.next()
                            pend = []
                            for f in range(16):
                                pt, Bp = pp.next()
                                for c in range(2):
                                    P.op("pe", MM(pt[:, 0:256], wpl[:, c, f * 128:(f + 1) * 128], pT[:, c, ts_], c == 0, c == 1),
                                         reads=[Bwpl, BpT[2 * t], BpT[2 * t + 1]], writes=[Bp], sig=(c == 1))
                                while len(pend) > 1:
                                    pend.pop(0)()
                                P.op("act", ACTF(epre[:, f, :], pt[:, 0:256], AF.Identity), reads=[Bp], writes=[Bep])
                                sq, Bsq = sqp.next()
                                P.op("act", ACTF(sq[:], pt[:, 0:256], AF.Square), reads=[Bp], writes=[Bsq])
                                pend.append(lambda sq=sq, Bsq=Bsq, f=f, pq=pq, Bpq=Bpq: P.op("pe", MM(pq[:, 0:256], om2048_b[:], sq[:], f == 0, f == 15), reads=[Bsq, Bc], writes=[Bpq]))
                            while pend:
                                pend.pop(0)()
                            rs, Brs = rsp.next()
                            P.op("act", ACTF(rs[:], pq[:, 0:256], AF.Ln, bias=epsb[:, 0:1]), reads=[Bpq, Bc], writes=[Brs])
                            P.op("act", ACTF(rs[:], rs[:], AF.Exp, scale=-0.5), reads=[Brs], writes=[Brs])
                            for f in range(16):
                                P.op("dve", STT(eall[:, f, ts_], epre[:, f, :], vt[:, pg + f:pg + f + 1], rs[:], ALU.mult, ALU.mult),
                                     reads=[Bep, Bvt, Brs], writes=[Bea[t]])
                        for f in range(16):
                            w_, Bw = wgp.next()
                            P.dma("pool", DMA(w_[:], w_ple_gate[l, f]), Bw, writes=[Bw])
                            for t in range(4):
                                ts_ = slice(t * 512, (t + 1) * 512)
                                hsl = SC['hT'][f * 128:(f + 1) * 128, t0 + t * 512:t0 + (t + 1) * 512]
                                hi, Bhi = hip.next()
                                P.dma("sp", DMA(hi[:], hsl), Bhi, writes=[Bhi])
                                pt, Bp = pp.next()
                                for c in range(16):
                                    P.op("pe", MM(pt[:], w_[:, c, :], hmb[:, c, ts_], c == 0, c == 15), reads=[Bw, Bhm[c][t]], writes=[Bp], sig=(c == 15))
                                sg, Bsg = sgp.next()
                                P.op("act", ACTF(sg[:], pt[:], AF.Sigmoid), reads=[Bp], writes=[Bsg])
                                P.op("dve", TT(sg[:], sg[:], eall[:, f, ts_], ALU.mult), reads=[Bsg, Bea[2 * t], Bea[2 * t + 1]], writes=[Bsg])
                                ho, Bho = hop.next()
                                P.op("dve", TT(ho[:], sg[:], hi[:], ALU.add), reads=[Bsg, Bhi], writes=[Bho])
                                P.dma("sp", DMA(hsl, ho[:]), Bho, reads=[Bho])
                    P.new_phase()

        def phase7():
            with ExitStack() as st:
                hp = RPool(st, SBT, "o_h", 20, [128, 512], F32)
                yp = RPool(st, SBT, "o_y", 2, [128, D], F32)
                pp = RPool(st, PST, "o_ps", 4, [128, 512], F32)
                for g in range(8):
                    hc = []
                    for c in range(16):
                        h_, Bh = hp.next()
                        P.dma("sp", DMA(h_[:], SC['hT'][c * 128:(c + 1) * 128, g * 512:(g + 1) * 512]), Bh, writes=[Bh])
                        hc.append((h_, Bh))
                    for j in range(4):
                        yt, By = yp.next()
                        for cq in range(4):
                            pt, Bp = pp.next()
                            for k in range(4):
                                c = cq * 4 + k
                                P.op("pe", TR(pt[:, k * 128:(k + 1) * 128], hc[c][0][:, j * 128:(j + 1) * 128], ident_f[:]), reads=[hc[c][1], Bc], writes=[Bp], sig=(k == 3))
                            if cq % 2 == 0:
                                P.op("act", ACTF(yt[:, cq * 512:(cq + 1) * 512], pt[:], AF.Identity), reads=[Bp], writes=[By])
                            else:
                                P.op("dve", CP(yt[:, cq * 512:(cq + 1) * 512], pt[:]), reads=[Bp], writes=[By])
                        r0 = (g * 4 + j) * 128
                        P.dma("sp", DMA(y[r0:r0 + 128, :], yt[:]), By, reads=[By])
            P.new_phase()

        if 0 not in skip:
            phase0()
        if stop_after == 'p0':
            P.emit()
            return nc
        for l in range(n_layers):
            if 1 not in skip:
                phase1(l)
            else:
                P.dma("sp", DMA(vt[:], vec[l]), Bvt, writes=[Bvt])
                P.new_phase()
            if stop_after == 'p1':
                break
            if phases is None or 2 in phases:
                phase2(l)
            if phases is None or 3 in phases:
                phase3(l)
            if stop_after == 'p3':
                break
            if phases is None or 4 in phases:
                phase4(l, qts=(range(32) if qts is None else qts))
            if stop_after == 'p4':
                break
            if phases is None or 5 in phases:
                phase5(l, **fox_kw)
            if stop_after == 'p5':
                break
            if phases is None or 6 in phases:
                phase6(l, **p6_kw)
            if stop_after == 'p6':
                break
        if phases is None or 7 in phases:
            phase7()
        P.emit()
    return nc


def make_in_maps(inp, cores=range(8)):
    vec, rows = host_vecs(inp)
    cst = host_consts()
    def tile_sq(w):
        k = w.shape[1] // 128
        return np.ascontiguousarray(w.reshape(L, k, 128, 16, 128).transpose(0, 3, 2, 1, 4))
    wout_t = tile_sq(inp['w_out'])
    wg_t = tile_sq(inp['w_ple_gate'])
    wbr_t = np.ascontiguousarray(inp['w_branch'].reshape(L, 3, 8, 128, 16, 128).transpose(0, 1, 4, 3, 2, 5))
    maps = []
    for b in cores:
        m = dict(x=np.ascontiguousarray(inp['x'][b]), p=np.ascontiguousarray(inp['p'][:, b]),
                 w_in=inp['w_in'], vec=vec, rows=rows,
                 lru_wa=inp['lru_wa'], lru_wx=inp['lru_wx'], cmp_w1=inp['cmp_w1'], cmp_w2=inp['cmp_w2'],
                 w_branch=wbr_t, w_out=wout_t, w_ple=inp['w_ple'], w_ple_gate=wg_t,
                 c_ident=cst['ident'], c_tri=cst['tri'], c_triu=cst['triu'], c_ovl=cst['ovl'], c_cmask=cst['cmask'],
                 c_tkmul=cst['tkmul'], c_tkadd=cst['tkadd'], c_esel=cst['esel'])
        maps.append(m)
    return maps


_NC = None


def kernel(**inp):
    global _NC
    inp = {k: np.asarray(v) for k, v in inp.items()}
    if _NC is None:
        _NC = build()
    res = run_bass_kernel_spmd(_NC, make_in_maps(inp), core_ids=list(range(8)))
    return np.stack([np.asarray(r["y"]) for r in res.results], axis=0)
```

```python
import numpy as np
import concourse.bass as bass
import concourse.mybir as mybir
from concourse.bass_utils import run_bass_kernel_spmd
from contextlib import ExitStack

F32 = mybir.dt.float32
BF16 = mybir.dt.bfloat16
AF = mybir.ActivationFunctionType
ALU = mybir.AluOpType
AX = mybir.AxisListType

ENGS = ("pe", "act", "dve", "pool", "sp")


class Buf:
    __slots__ = ("name", "last_w", "readers", "sem", "sem_sw", "dma_count")

    def __init__(self, name):
        self.name = name
        self.last_w = None
        self.readers = []
        self.sem = None
        self.sem_sw = None
        self.dma_count = 0


class Prog:
    def __init__(self, nc, n_dma_sems=150):
        self.nc = nc
        self.ops = {e: [] for e in ENGS}
        self.nsig = {e: 0 for e in ENGS}
        self.dma_sem_total = {}
        self.next_dma_sem = {"h": 0, "s": 0}
        self.phase_base = {"h": 0, "s": 0}
        self.max_dma_sem = {"h": 0, "s": 0}
        self.nbar = 0

    def freeze_persistent(self):
        self.phase_base = dict(self.next_dma_sem)

    def new_phase(self):
        self.barrier()
        for k in "hs":
            self.max_dma_sem[k] = max(self.max_dma_sem[k], self.next_dma_sem[k])
        self.next_dma_sem = dict(self.phase_base)

    def _deps_for(self, eng, reads, writes, is_dma=False):
        deps = []
        for b in reads:
            d = b.last_w
            if d is not None and not (d[0] == "c" and d[1] == eng and eng == "pe"):
                deps.append(d)
        for b in writes:
            for d in [b.last_w] + b.readers:
                if d is None:
                    continue
                if d[0] == "c" and d[1] == eng and eng == "pe" and not is_dma:
                    continue
                deps.append(d)
        return deps

    def op(self, eng, fn, reads=(), writes=(), sig=True):
        deps = self._deps_for(eng, reads, writes)
        idx = len(self.ops[eng])
        ev = ("c", eng, idx)
        self.ops[eng].append(dict(fn=fn, deps=deps, sig=sig, dma=None))
        for b in reads:
            b.readers.append(ev)
        for b in writes:
            b.last_w = ev
            b.readers = []
        return ev

    def dma(self, queue, fn, sbuf, reads=(), writes=()):
        deps = self._deps_for(queue, reads, writes, is_dma=True)
        attr = "sem_sw" if queue == "pool" else "sem"
        kind = "s" if queue == "pool" else "h"
        if getattr(sbuf, attr) is None:
            setattr(sbuf, attr, (kind, self.next_dma_sem[kind]))
            self.next_dma_sem[kind] += 1
        si = getattr(sbuf, attr)
        self.dma_sem_total[si] = self.dma_sem_total.get(si, 0) + 16
        ev = ("d", si, self.dma_sem_total[si])
        self.ops[queue].append(dict(fn=fn, deps=deps, sig=False, dma=si))
        for b in reads:
            b.readers.append(ev)
        for b in writes:
            b.last_w = ev
            b.readers = []
        return ev

    def barrier(self):
        self.nbar += 1
        n = self.nbar
        dma_tot = dict(self.dma_sem_total)
        marks = {e: len(self.ops[e]) for e in ENGS}
        for e in ENGS:
            self.ops[e].append(dict(bar=n, marks=marks, dma_tot=dma_tot))

    def emit(self):
        nc = self.nc
        from contextlib import ExitStack
        with ExitStack() as st:
            esem = {e: st.enter_context(nc.semaphore("cnt_" + e)) for e in ENGS}
            dsem = {}
            for k in "hs":
                n = max(self.next_dma_sem[k], self.max_dma_sem[k])
                for i in range(n):
                    dsem[(k, i)] = st.enter_context(nc.semaphore("dma%s%d" % (k, i)))
            bsem = st.enter_context(nc.semaphore("bar"))
            block = st.enter_context(nc.Block())
            for e in ENGS:
                ops = self.ops[e]
                for i, o in enumerate(ops):
                    if "bar" in o:
                        j = i - 1
                        while j >= 0 and ("bar" in ops[j] or ops[j]["dma"] is not None):
                            j -= 1
                        if j >= 0:
                            ops[j]["sig"] = True
                j = len(ops) - 1
                while j >= 0 and ("bar" in ops[j] or ops[j]["dma"] is not None):
                    j -= 1
                if j >= 0:
                    ops[j]["sig"] = True
            ticket = {}
            for e in ENGS:
                ops = self.ops[e]
                t = [0] * (len(ops) + 1)
                cnt = 0
                for i, o in enumerate(ops):
                    if o.get("sig"):
                        cnt += 1
                    t[i] = cnt
                res = [0] * len(ops)
                nxt = None
                for i in range(len(ops) - 1, -1, -1):
                    if ops[i].get("sig"):
                        nxt = t[i]
                    res[i] = nxt
                ticket[e] = res
                pre = [0] * (len(ops) + 1)
                cnt = 0
                for i, o in enumerate(ops):
                    pre[i] = cnt
                    if o.get("sig"):
                        cnt += 1
                pre[len(ops)] = cnt
                ticket[e + "_pre"] = pre

            def run(e, eng):
                waited = {}

                def wait(sem, key, val):
                    if val is None or val <= 0:
                        return
                    if waited.get(key, 0) >= val:
                        return
                    waited[key] = val
                    eng.wait_ge(sem, val)

                for o in self.ops[e]:
                    if "bar" in o:
                        n = o["bar"]
                        if e == "sp":
                            for si, tot in o["dma_tot"].items():
                                wait(dsem[si], ("d", si), tot)
                            for f in ENGS:
                                if f != "sp":
                                    wait(esem[f], ("c", f), ticket[f + "_pre"][o["marks"][f]])
                            eng.sem_inc(bsem, 1)
                        else:
                            wait(bsem, ("b",), n)
                        for si, tot in o["dma_tot"].items():
                            waited[("d", si)] = max(waited.get(("d", si), 0), tot)
                        for f in ENGS:
                            waited[("c", f)] = max(waited.get(("c", f), 0), ticket[f + "_pre"][o["marks"][f]])
                        continue
                    for d in o["deps"]:
                        if d[0] == "c":
                            wait(esem[d[1]], ("c", d[1]), ticket[d[1]][d[2]])
                        else:
                            wait(dsem[d[1]], ("d", d[1]), d[2])
                    inst = o["fn"](eng)
                    if o["dma"] is not None:
                        inst.then_inc(dsem[o["dma"]], 16)
                    elif o["sig"]:
                        inst.then_inc(esem[e], 1)

            @block.tensor
            def _(eng):
                run("pe", eng)

            @block.scalar
            def _(eng):
                run("act", eng)

            @block.vector
            def _(eng):
                run("dve", eng)

            @block.gpsimd
            def _(eng):
                run("pool", eng)

            @block.sync
            def _(eng):
                run("sp", eng)


D = 2048
S = 4096
L = 4
NIN = 15904
EPS = 1e-6
SPLITS = (('lru_x', 1024), ('lru_gate', 1024), ('nsa_q', 1024), ('nsa_k_cmp', 256), ('nsa_v_cmp', 256),
          ('nsa_k_slc', 256), ('nsa_v_slc', 256), ('nsa_k_win', 256), ('nsa_v_win', 256),
          ('nsa_bgate', 24), ('nsa_gate', 1024), ('fox_q', 1024), ('fox_k', 1024), ('fox_v', 1024),
          ('fox_f', 8), ('fox_gate', 1024), ('merge', 6144))
OFF = {}
_s = 0
for _n, _z in SPLITS:
    OFF[_n] = (_s, _z)
    _s += _z

P1_JOBS = [
    ('lru_x', 'fm', 'zlx', dict(func='ident', odt='f32')),
    ('lru_gate', 'fm', 'zlg', dict(func='silu')),
    ('nsa_q', 'hn', 'qn', dict(gain='nsa_q_gain', qscale=True)),
    ('nsa_k_cmp', 'fm', 'kci', dict(func='ident')),
    ('nsa_v_cmp', 'fm', 'vci', dict(func='ident')),
    ('nsa_k_slc', 'hn', 'ksl', dict(gain='nsa_k_gain')),
    ('nsa_v_slc', 'tm', 'vsl', dict()),
    ('nsa_k_win', 'hn', 'kwn', dict(gain='nsa_k_gain')),
    ('nsa_v_win', 'tm', 'vwn', dict()),
    ('nsa_bgate', 'tm', 'bg', dict(func='sigmoid', odt='f32')),
    ('nsa_gate', 'fm', 'ng', dict(func='silu')),
    ('fox_q', 'hn', 'fq', dict(gain='fox_q_gain', qscale=True)),
    ('fox_k', 'hn', 'fk', dict(gain='fox_k_gain')),
    ('fox_v', 'tm', 'fv', dict()),
    ('fox_f', 'fm', 'ff', dict(func='ident', odt='f32')),
    ('fox_gate', 'fm', 'fg', dict(func='silu')),
    ('merge', 'fm', 'mg', dict(func='sigmoid')),
]


def vec_layout():
    cols = {}
    n = 0

    def add(name, k):
        nonlocal n
        cols[name] = n
        n += k
    add('ln_gain', 16)
    for name, kind, dst, o in P1_JOBS:
        if kind in ('fm', 'hn'):
            add('b_' + name, (OFF[name][1] + 127) // 128)
    add('conv_w', 32)
    add('conv_b', 8)
    add('lru_ba', 8)
    add('lru_bx', 8)
    add('lru_lambda', 8)
    add('nsa_q_gain', 1)
    add('nsa_k_gain', 1)
    add('fox_q_gain', 1)
    add('fox_k_gain', 1)
    add('ple_gain', 16)
    add('cmp_pos', 64)
    return cols, n


def row_layout():
    cols = {}
    n = 0
    for name, kind, dst, o in P1_JOBS:
        if kind == 'tm':
            cols[name] = n
            n += OFF[name][1]
    return cols, n


VCOL, NV = vec_layout()
RCOL, NR = row_layout()


def host_vecs(inp):
    vec = np.zeros((L, 128, NV), np.float32)
    rows = np.zeros((L, NR), np.float32)
    for l in range(L):
        def put(name, v):
            v = np.asarray(v, np.float32).reshape(-1)
            k = (v.size + 127) // 128
            buf = np.zeros(k * 128, np.float32)
            buf[:v.size] = v
            vec[l, :, VCOL[name]:VCOL[name] + k] = buf.reshape(k, 128).T
        put('ln_gain', inp['ln_gain'][l])
        for name, kind, dst, o in P1_JOBS:
            c0, n = OFF[name]
            if kind in ('fm', 'hn'):
                put('b_' + name, inp['b_in'][l, c0:c0 + n])
            else:
                rows[l, RCOL[name]:RCOL[name] + n] = inp['b_in'][l, c0:c0 + n]
        cw = inp['conv_w'][l]
        vec[l, :, VCOL['conv_w']:VCOL['conv_w'] + 32] = cw.reshape(4, 8, 128).transpose(2, 0, 1).reshape(128, 32)
        put('conv_b', inp['conv_b'][l])
        put('lru_ba', inp['lru_ba'][l])
        put('lru_bx', inp['lru_bx'][l])
        put('lru_lambda', inp['lru_lambda'][l])
        for g in ('nsa_q_gain', 'nsa_k_gain', 'fox_q_gain', 'fox_k_gain'):
            put(g, inp[g][l])
        put('ple_gain', inp['ple_gain'][l])
        cp = inp['cmp_pos'][l]
        vec[l, :, VCOL['cmp_pos']:VCOL['cmp_pos'] + 64] = cp.reshape(64, 128).T
    return vec, rows


def host_consts():
    c = {}
    c['ident'] = np.eye(128, dtype=np.float32)
    k = np.arange(128)[:, None]
    q = np.arange(128)[None, :]
    c['tri'] = (k <= q).astype(np.float32)
    c['triu'] = (k > q).astype(np.float32)
    cs = np.arange(255) * 16
    ss = np.arange(64) * 64
    ov = ((cs[:, None] < ss[None, :] + 64) & (cs[:, None] + 32 > ss[None, :])).astype(np.float32)
    ovl = np.zeros((256, 65), np.float32)
    ovl[:255, :64] = ov
    ovl[:255, 64] = 1.0
    c['ovl'] = ovl
    cm = np.zeros((128, 17, 128), np.float32)
    for i in range(17):
        cm[:, i, :] = (128 * i + q - 16 * k >= 31)
    c['cmask'] = cm
    ql = np.arange(128)[:, None, None]
    qt = np.arange(32)[None, :, None]
    j = np.arange(64)[None, None, :]
    cur = 2 * qt + (ql >= 64)
    forced = (j == 0) | (j == cur) | (j == cur - 1)
    valid = j <= cur
    c['tkmul'] = (valid & ~forced).astype(np.float32) * np.ones((128, 32, 64), np.float32)
    c['tkadd'] = np.where(forced, 1e6 + j, np.where(valid, 0.0, -1e6 - j)).astype(np.float32) * np.ones((128, 32, 64), np.float32)
    jj = np.arange(64)[:, None, None]
    kc = np.arange(32)[None, :, None]
    kk = np.arange(128)[None, None, :]
    c['esel'] = (jj == 2 * kc + (kk >= 64)).astype(np.float32)
    return c


def MM(out, lhsT, rhs, start, stop, skip=False):
    if skip:
        return lambda e: e.matmul(out, lhsT=lhsT, rhs=rhs, start=start, stop=stop, skip_group_check=True)
    return lambda e: e.matmul(out, lhsT=lhsT, rhs=rhs, start=start, stop=stop)


def TR(out, in_, ident):
    return lambda e: e.transpose(out=out, in_=in_, identity=ident)


def ACTF(out, in_, func, bias=None, scale=None, accum_out=None):
    kw = {}
    if bias is not None:
        kw['bias'] = bias
    if scale is not None:
        kw['scale'] = scale
    if accum_out is not None:
        kw['accum_out'] = accum_out
    return lambda e: e.activation(out=out, in_=in_, func=func, **kw)


def TS(out, in0, s1, s2, op0, op1=None):
    if op1 is None:
        return lambda e: e.tensor_scalar(out=out, in0=in0, scalar1=s1, scalar2=None, op0=op0)
    return lambda e: e.tensor_scalar(out=out, in0=in0, scalar1=s1, scalar2=s2, op0=op0, op1=op1)


def STT(out, in0, scalar, in1, op0, op1):
    return lambda e: e.scalar_tensor_tensor(out=out, in0=in0, scalar=scalar, in1=in1, op0=op0, op1=op1)


def TT(out, in0, in1, op):
    return lambda e: e.tensor_tensor(out=out, in0=in0, in1=in1, op=op)


def CP(out, in_):
    return lambda e: e.tensor_copy(out=out, in_=in_)


def DMA(out, in_):
    return lambda e: e.dma_start(out=out, in_=in_)


FUNCS = {'ident': AF.Identity, 'silu': AF.Silu, 'sigmoid': AF.Sigmoid}


class RPool:
    def __init__(self, st, alloc, name, n, shape, dt):
        self.items = []
        for i in range(n):
            t = st.enter_context(alloc("%s%d" % (name, i), shape, dt))
            self.items.append((t, Buf("%s%d" % (name, i))))
        self.i = 0

    def next(self):
        it = self.items[self.i % len(self.items)]
        self.i += 1
        return it


def build(n_layers=L, dbg=(), stop_after=None, phases=None, skip=(), qts=None, fox_kw={}, p6_kw={}):
    nc = bass.Bass("TRN2", target_bir_lowering=False)
    P = Prog(nc)
    _uid = [0]

    def SBT(name, shape, dt):
        _uid[0] += 1
        return nc.sbuf_tensor("%s_%d" % (name, _uid[0]), shape, dt)

    def PST(name, shape, dt):
        _uid[0] += 1
        return nc.psum_tensor("%s_%d" % (name, _uid[0]), shape, dt)

    def din(name, shape, dt=F32):
        return nc.dram_tensor(name, list(shape), dt, kind="ExternalInput").ap()

    x = din("x", [S, D])
    p_in = din("p", [L, S, 256])
    w_in = din("w_in", [L, D, NIN])
    vec = din("vec", [L, 128, NV])
    rows = din("rows", [L, NR])
    c_ident = din("c_ident", [128, 128])
    c_tri = din("c_tri", [128, 128])
    c_triu = din("c_triu", [128, 128])
    lru_wa = din("lru_wa", [L, 16, 64, 64])
    lru_wx = din("lru_wx", [L, 16, 64, 64])
    cmp_w1 = din("cmp_w1", [L, 2, 4096, 256])
    cmp_w2 = din("cmp_w2", [L, 2, 256, 128])
    c_ovl = din("c_ovl", [256, 65])
    c_cmask = din("c_cmask", [128, 17, 128])
    c_tkmul = din("c_tkmul", [128, 32, 64])
    c_tkadd = din("c_tkadd", [128, 32, 64])
    c_esel = din("c_esel", [64, 32, 128])
    w_branch = din("w_branch", [L, 3, 16, 128, 8, 128])
    w_out = din("w_out", [L, 16, 128, 16, 128])
    w_ple = din("w_ple", [L, 256, D])
    w_ple_gate = din("w_ple_gate", [L, 16, 128, 16, 128])
    y = nc.dram_tensor("y", [S, D], F32, kind="ExternalOutput").ap()

    def scratch(name, shape, dt):
        kind = "ExternalOutput" if name in dbg else "Internal"
        return nc.dram_tensor(name, list(shape), dt, kind=kind).ap()

    SC = {}
    SC['hT'] = scratch('hT', [D, S], F32)
    SC['zlx'] = scratch('zlx', [1024, S], F32)
    SC['zlg'] = scratch('zlg', [1024, S], BF16)
    SC['qn'] = scratch('qn', [1024, S], BF16)
    SC['kci'] = scratch('kci', [256, S], BF16)
    SC['vci'] = scratch('vci', [256, S], BF16)
    SC['ksl'] = scratch('ksl', [256, S], BF16)
    SC['kwn'] = scratch('kwn', [256, S], BF16)
    SC['vsl'] = scratch('vsl', [S, 256], BF16)
    SC['vwn'] = scratch('vwn', [S, 256], BF16)
    SC['bg'] = scratch('bg', [S, 24], F32)
    SC['ng'] = scratch('ng', [1024, S], BF16)
    SC['fq'] = scratch('fq', [1024, S], BF16)
    SC['fk'] = scratch('fk', [1024, S], BF16)
    SC['fv'] = scratch('fv', [S, 1024], BF16)
    SC['ff'] = scratch('ff', [8, S], F32)
    SC['fg'] = scratch('fg', [1024, S], BF16)
    SC['mg'] = scratch('mg', [6144, S], BF16)
    SC['yaT'] = scratch('yaT', [1024, S], BF16)
    SC['ybT'] = scratch('ybT', [1024, S], BF16)
    SC['ycT'] = scratch('ycT', [1024, S], BF16)
    SC['hs'] = scratch('hs', [1024, S], F32)
    SC['cf'] = scratch('cf', [8, S], F32)
    SC['mgd'] = scratch('mgd', [D, S], BF16)
    if 'kcd' in dbg:
        SC['kcd'] = scratch('kcd', [128, 2, 256], BF16)
        SC['vcd'] = scratch('vcd', [128, 2, 2, 193], BF16)

    with ExitStack() as gst:
        def gsb(name, shape, dt):
            return gst.enter_context(SBT(name, shape, dt))
        ident_f = gsb("ident_f", [128, 128], F32)
        ident_b = gsb("ident_b", [128, 128], BF16)
        tri_b = gsb("tri_b", [128, 128], BF16)
        triu_b = gsb("triu_b", [128, 128], BF16)
        ones_b = gsb("ones_b", [128, 128], BF16)
        om128_b = gsb("om128_b", [128, 128], BF16)
        om2048_b = gsb("om2048_b", [128, 128], BF16)
        Bc = Buf("consts")
        P.dma("sp", DMA(ident_f[:], c_ident), Bc, writes=[Bc])
        P.dma("pool", DMA(ident_b[:], c_ident), Bc, writes=[Bc])
        P.dma("pool", DMA(tri_b[:], c_tri), Bc, writes=[Bc])
        P.dma("pool", DMA(triu_b[:], c_triu), Bc, writes=[Bc])
        P.op("dve", lambda e: e.memset(ones_b[:], 1.0), writes=[Bc])
        P.op("dve", lambda e: e.memset(om128_b[:], 1.0 / 128), writes=[Bc])
        P.op("dve", lambda e: e.memset(om2048_b[:], 1.0 / 2048), writes=[Bc])
        epsb = gsb("epsb", [128, 2], F32)
        P.op("dve", lambda e: e.memset(epsb[:, 0:1], EPS), writes=[Bc])
        P.op("dve", lambda e: e.memset(epsb[:, 1:2], EPS * 128.0), writes=[Bc])
        vt = gsb("vt", [128, NV], F32)
        kcT = gsb("kcT", [128, 2, 256], BF16)
        vco = gsb("vco", [128, 2, 2, 193], BF16)
        Bkc = Buf("kcT")
        Bvco = Buf("vco")
        one_f = gsb("one_f", [128, 1], F32)
        P.op("dve", lambda e: e.memset(one_f[:], 1.0), writes=[Bc])
        Bvt = Buf("vt")
        P.freeze_persistent()
        P.new_phase()

        def phase0():
            with ExitStack() as st:
                def sb(name, shape, dt):
                    return SBT(name, shape, dt)

                def ps(name, shape, dt):
                    return PST(name, shape, dt)
                xp = RPool(st, sb, "p0x", 8, [128, D], F32)
                pp = RPool(st, ps, "p0ps", 4, [128, 512], F32)
                sp_ = RPool(st, sb, "p0st", 4, [128, 512], F32)
                for g in range(8):
                    xt = []
                    for j in range(4):
                        t, B = xp.next()
                        r0 = (g * 4 + j) * 128
                        P.dma("sp", DMA(t[:], x[r0:r0 + 128, :]), B, writes=[B])
                        xt.append((t, B))
                    for c in range(16):
                        pt, Bp = pp.next()
                        for j in range(4):
                            P.op("pe", TR(pt[:, j * 128:(j + 1) * 128], xt[j][0][:, c * 128:(c + 1) * 128], ident_f[:]),
                                 reads=[xt[j][1], Bc], writes=[Bp], sig=(j == 3))
                        s_, Bs = sp_.next()
                        eng = "act" if c % 2 == 0 else "dve"
                        if eng == "act":
                            P.op("act", ACTF(s_[:], pt[:], AF.Identity), reads=[Bp], writes=[Bs])
                        else:
                            P.op("dve", CP(s_[:], pt[:]), reads=[Bp], writes=[Bs])
                        P.dma("sp", DMA(SC['hT'][c * 128:(c + 1) * 128, g * 512:(g + 1) * 512], s_[:]), Bs, reads=[Bs])
            P.new_phase()

        def phase1(l):
            with ExitStack() as st:
                def sb(name, shape, dt):
                    return SBT(name, shape, dt)

                def ps(name, shape, dt):
                    return PST(name, shape, dt)
                P.dma("sp", DMA(vt[:], vec[l]), Bvt, writes=[Bvt])
                hn = [st.enter_context(sb("hn%d" % c, [128, S], BF16)) for c in range(16)]
                Bhn = [[Buf("hn%d_%d" % (c, t)) for t in range(8)] for c in range(16)]
                hp = RPool(st, sb, "p1h", 16, [128, 512], F32)
                sqp = RPool(st, sb, "p1sq", 3, [128, 512], BF16)
                zbp = RPool(st, sb, "p1zb", 3, [128, 512], F32)
                rsp = RPool(st, sb, "p1rs", 2, [128, 512], F32)
                wp = RPool(st, sb, "p1w", 2, [128, 16, 256], BF16)
                obp = RPool(st, sb, "p1ob", 4, [128, 512], BF16)
                ofp = RPool(st, sb, "p1of", 2, [128, 512], F32)
                brp = RPool(st, sb, "p1br", 2, [128, 256], F32)
                pm = RPool(st, ps, "p1pm", 5, [128, 512], F32)
                p2 = RPool(st, ps, "p1p2", 2, [128, 512], F32)
                for t in range(8):
                    ts_ = slice(t * 512, (t + 1) * 512)
                    hts = []
                    for c in range(16):
                        h_, Bh = hp.next()
                        P.dma("sp", DMA(h_[:], SC['hT'][c * 128:(c + 1) * 128, ts_]), Bh, writes=[Bh])
                        hts.append((h_, Bh))
                    pst, Bpst = p2.next()
                    for c in range(16):
                        sq, Bsq = sqp.next()
                        P.op("act", ACTF(sq[:], hts[c][0][:], AF.Square), reads=[hts[c][1]], writes=[Bsq])
                        P.op("pe", MM(pst[:], om2048_b[:], sq[:], c == 0, c == 15), reads=[Bsq, Bc], writes=[Bpst])
                    rs, Brs = rsp.next()
                    P.op("act", ACTF(rs[:], pst[:], AF.Ln, bias=epsb[:, 0:1]), reads=[Bpst, Bc], writes=[Brs])
                    P.op("act", ACTF(rs[:], rs[:], AF.Exp, scale=-0.5), reads=[Brs], writes=[Brs])
                    for c in range(16):
                        eng = "dve"
                        P.op(eng, STT(hn[c][:, ts_], hts[c][0][:], vt[:, VCOL['ln_gain'] + c:VCOL['ln_gain'] + c + 1], rs[:], ALU.mult, ALU.mult),
                             reads=[hts[c][1], Bvt, Brs], writes=[Bhn[c][t]])
                pending = []

                def flush():
                    while pending:
                        pending.pop(0)()
                for name, kind, dst, o in P1_JOBS:
                    c0, n = OFF[name]
                    odt = F32 if o.get('odt') == 'f32' else BF16
                    for b0 in range(0, n, 256):
                        bn = min(256, n - b0)
                        wt, Bw = wp.next()
                        P.dma("pool", DMA(wt[:, :, :bn], w_in[l, :, c0 + b0:c0 + b0 + bn].rearrange("(c p) n -> p c n", p=128)), Bw, writes=[Bw])
                        if kind == 'tm':
                            br, Bbr = brp.next()
                            P.dma("sp", DMA(br[:, :bn], rows[l:l + 1, RCOL[name] + b0:RCOL[name] + b0 + bn].partition_broadcast(128)), Bbr, writes=[Bbr])
                            for tt in range(32):
                                pt, Bp = pm.next()
                                for c in range(16):
                                    P.op("pe", MM(pt[:, :bn], hn[c][:, tt * 128:(tt + 1) * 128], wt[:, c, :bn], c == 0, c == 15),
                                         reads=[Bw, Bhn[c][tt // 4]], writes=[Bp], sig=(c == 15))
                                flush()
                                if odt == F32:
                                    ob, Bo = ofp.next()
                                else:
                                    ob, Bo = obp.next()
                                if o.get('func') == 'sigmoid':
                                    zb, Bz = zbp.next()
                                    P.op("dve", TT(zb[:, :bn], pt[:, :bn], br[:, :bn], ALU.add), reads=[Bp, Bbr], writes=[Bz])
                                    P.op("act", ACTF(ob[:, :bn], zb[:, :bn], AF.Sigmoid), reads=[Bz], writes=[Bo])
                                else:
                                    P.op("dve", TT(ob[:, :bn], pt[:, :bn], br[:, :bn], ALU.add), reads=[Bp, Bbr], writes=[Bo])
                                P.dma("sp", DMA(SC[dst][tt * 128:(tt + 1) * 128, b0:b0 + bn], ob[:, :bn]), Bo, reads=[Bo])
                            continue
                        for m0 in range(0, bn, 128):
                            mm = min(128, bn - m0)
                            ch = (b0 + m0) // 128
                            bcol = VCOL['b_' + name] + ch
                            bias = vt[:mm, bcol:bcol + 1]
                            for t in range(8):
                                ts_ = slice(t * 512, (t + 1) * 512)
                                pt, Bp = pm.next()
                                for c in range(16):
                                    P.op("pe", MM(pt[:mm, :], wt[:, c, m0:m0 + mm], hn[c][:, ts_], c == 0, c == 15),
                                         reads=[Bw, Bhn[c][t]], writes=[Bp], sig=(c == 15))
                                flush()
                                drow = SC[dst][b0 + m0:b0 + m0 + mm, ts_]
                                if kind == 'fm':
                                    if odt == F32:
                                        ob, Bo = ofp.next()
                                    else:
                                        ob, Bo = obp.next()
                                    P.op("act", ACTF(ob[:mm, :], pt[:mm, :], FUNCS[o['func']], bias=bias), reads=[Bp, Bvt], writes=[Bo])
                                    P.dma("sp", DMA(drow, ob[:mm, :]), Bo, reads=[Bo])
                                else:
                                    zb, Bz = zbp.next()
                                    sq, Bsq = sqp.next()
                                    P.op("act", ACTF(zb[:], pt[:], AF.Identity, bias=bias), reads=[Bp, Bvt], writes=[Bz])
                                    P.op("act", ACTF(sq[:], pt[:], AF.Square, bias=bias), reads=[Bp, Bvt], writes=[Bsq])
                                    gcol = VCOL[o['gain']]
                                    qs = o.get('qscale', False)

                                    def post(zb=zb, Bz=Bz, sq=sq, Bsq=Bsq, gcol=gcol, qs=qs, drow=drow):
                                        pq, Bpq = p2.next()
                                        P.op("pe", MM(pq[:], (ones_b if qs else om128_b)[:], sq[:], True, True), reads=[Bsq, Bc], writes=[Bpq])
                                        rs, Brs = rsp.next()
                                        P.op("act", ACTF(rs[:], pq[:], AF.Ln, bias=epsb[:, (1 if qs else 0):(2 if qs else 1)]), reads=[Bpq, Bc], writes=[Brs])
                                        P.op("act", ACTF(rs[:], rs[:], AF.Exp, scale=-0.5), reads=[Brs], writes=[Brs])
                                        ob, Bo = obp.next()
                                        P.op("dve", STT(ob[:], zb[:], vt[:, gcol:gcol + 1], rs[:], ALU.mult, ALU.mult), reads=[Bz, Bvt, Brs], writes=[Bo])
                                        P.dma("sp", DMA(drow, ob[:]), Bo, reads=[Bo])
                                    pending.append(post)
                flush()
            P.new_phase()


        def phase2(l):
            H = S // 2
            with ExitStack() as st:
                def sb(name, shape, dt):
                    return st.enter_context(SBT(name, shape, dt))
                sets = []
                for k in range(2):
                    d = {}
                    for nm, shape, dt in (("up", [128, H + 3], F32), ("uc", [128, H], F32), ("ucb", [128, H], BF16), ("ra", [128, H], F32),
                                          ("ix", [128, H], F32), ("t1", [128, H], F32), ("hs", [128, H], F32), ("gt", [128, H], BF16), ("ya", [128, H], BF16)):
                        d[nm] = sb("l_%s%d" % (nm, k), shape, dt)
                        d["B" + nm] = Buf("%s%d" % (nm, k))
                    sets.append(d)
                wab = [sb("l_wa%d" % k, [128, 128], BF16) for k in range(2)]; Bwa = [Buf("wa%d" % k) for k in range(2)]
                wxb = [sb("l_wx%d" % k, [128, 128], BF16) for k in range(2)]; Bwx = [Buf("wx%d" % k) for k in range(2)]
                cv = sb("l_cv", [128, 8], F32); Bcv = Buf("cv")
                pp = RPool(st, PST, "l_ps", 4, [128, 512], F32)
                lc = VCOL['lru_lambda']
                P.op("act", ACTF(cv[:], vt[:, lc:lc + 8], AF.Exp, scale=-1.0), reads=[Bvt], writes=[Bcv])
                P.op("act", ACTF(cv[:], cv[:], AF.Ln, bias=one_f[:, 0:1]), reads=[Bcv, Bc], writes=[Bcv])
                P.op("dve", TS(cv[:], cv[:], -8.0, None, ALU.mult), reads=[Bcv], writes=[Bcv])
                def issue_loads(m):
                    if m >= 16:
                        return
                    c_, hf_ = m // 2, m % 2
                    d_ = sets[m % 2]
                    r_ = slice(c_ * 128, (c_ + 1) * 128)
                    if hf_ == 0:
                        P.op("dve", lambda e, up=d_["up"]: e.memset(up[:, 0:3], 0.0), writes=[d_["Bup"]])
                        P.dma("sp", DMA(d_["up"][:, 3:], SC['zlx'][r_, 0:H]), d_["Bup"], writes=[d_["Bup"]])
                    else:
                        P.dma("sp", DMA(d_["up"][:], SC['zlx'][r_, H - 3:S]), d_["Bup"], writes=[d_["Bup"]])
                    P.dma("sp", DMA(d_["gt"][:], SC['zlg'][r_, hf_ * H:hf_ * H + H]), d_["Bgt"], writes=[d_["Bgt"]])
                n = 0
                for c in range(8):
                    rows_ = slice(c * 128, (c + 1) * 128)
                    wk = c % 2
                    P.op("pool", lambda e, wk=wk: e.memset(wab[wk][:], 0.0), writes=[Bwa[wk]])
                    P.op("pool", lambda e, wk=wk: e.memset(wxb[wk][:], 0.0), writes=[Bwx[wk]])
                    for j in range(2):
                        P.dma("pool", DMA(wab[wk][j * 64:(j + 1) * 64, j * 64:(j + 1) * 64], lru_wa[l, 2 * c + j]), Bwa[wk], writes=[Bwa[wk]])
                        P.dma("pool", DMA(wxb[wk][j * 64:(j + 1) * 64, j * 64:(j + 1) * 64], lru_wx[l, 2 * c + j]), Bwx[wk], writes=[Bwx[wk]])
                    prev = None
                    for hf in range(2):
                        d = sets[n % 2]
                        n += 1
                        tb = hf * H
                        up, uc, ucb, ra, ix, t1, hsb, gt, ya = (d[k] for k in ("up", "uc", "ucb", "ra", "ix", "t1", "hs", "gt", "ya"))
                        Bup, Buc, Bucb, Bra, Bix, Bt1, Bhs, Bgt, Bya = (d["B" + k] for k in ("up", "uc", "ucb", "ra", "ix", "t1", "hs", "gt", "ya"))
                        if n == 1:
                            issue_loads(0)
                        issue_loads(n)
                        cw = VCOL['conv_w']
                        cb = VCOL['conv_b'] + c
                        P.op("dve", TS(uc[:], up[:, 0:H], vt[:, cw + c:cw + c + 1], vt[:, cb:cb + 1], ALU.mult, ALU.add), reads=[Bup, Bvt], writes=[Buc])
                        for j in range(1, 4):
                            P.op("dve", STT(uc[:], up[:, j:j + H], vt[:, cw + 8 * j + c:cw + 8 * j + c + 1], uc[:], ALU.mult, ALU.add), reads=[Bup, Bvt, Buc], writes=[Buc])
                        P.op("act", ACTF(ucb[:], uc[:], AF.Identity), reads=[Buc], writes=[Bucb])
                        ba = VCOL['lru_ba'] + c
                        bx = VCOL['lru_bx'] + c
                        for t in range(H // 512):
                            ts_ = slice(t * 512, (t + 1) * 512)
                            p1, Bp1 = pp.next()
                            P.op("pe", MM(p1[:], wab[wk][:], ucb[:, ts_], True, True), reads=[Bwa[wk], Bucb], writes=[Bp1])
                            P.op("act", ACTF(ra[:, ts_], p1[:], AF.Sigmoid, bias=vt[:, ba:ba + 1]), reads=[Bp1, Bvt], writes=[Bra])
                            p2_, Bp2 = pp.next()
                            P.op("pe", MM(p2_[:], wxb[wk][:], ucb[:, ts_], True, True), reads=[Bwx[wk], Bucb], writes=[Bp2])
                            P.op("act", ACTF(ix[:, ts_], p2_[:], AF.Sigmoid, bias=vt[:, bx:bx + 1]), reads=[Bp2, Bvt], writes=[Bix])
                        P.op("act", ACTF(ra[:], ra[:], AF.Exp, scale=cv[:, c:c + 1]), reads=[Bra, Bcv], writes=[Bra])
                        P.op("pool", TT(ix[:], ix[:], uc[:], ALU.mult), reads=[Bix, Buc], writes=[Bix])
                        P.op("act", ACTF(t1[:], ra[:], AF.Square), reads=[Bra], writes=[Bt1])
                        P.op("act", ACTF(t1[:], t1[:], AF.Ln, bias=one_f[:, 0:1], scale=-1.0), reads=[Bt1, Bc], writes=[Bt1])
                        P.op("act", ACTF(t1[:], t1[:], AF.Exp, scale=0.5), reads=[Bt1], writes=[Bt1])
                        P.op("dve", TT(ix[:], ix[:], t1[:], ALU.mult), reads=[Bix, Bt1], writes=[Bix])
                        if prev is None:
                            P.op("dve", lambda e, hsb=hsb, ra=ra, ix=ix: e.tensor_tensor_scan(out=hsb[:], data0=ra[:], data1=ix[:], initial=0.0, op0=ALU.mult, op1=ALU.add),
                                 reads=[Bra, Bix], writes=[Bhs])
                        else:
                            ph, Bph = prev
                            P.op("dve", lambda e, hsb=hsb, ra=ra, ix=ix, ph=ph: e.tensor_tensor_scan(out=hsb[:], data0=ra[:], data1=ix[:], initial=ph[:, H - 1:H], op0=ALU.mult, op1=ALU.add),
                                 reads=[Bra, Bix, Bph], writes=[Bhs])
                        prev = (hsb, Bhs)
                        P.op("pool", TT(ya[:], hsb[:], gt[:], ALU.mult), reads=[Bhs, Bgt], writes=[Bya])
                        P.dma("sp", DMA(SC['yaT'][rows_, tb:tb + H], ya[:]), Bya, reads=[Bya])
                        if 'hs' in dbg:
                            P.dma("sp", DMA(SC['hs'][rows_, tb:tb + H], hsb[:]), Bhs, reads=[Bhs])
            P.new_phase()

        def phase3(l):
            with ExitStack() as st:
                def sb(name, shape, dt):
                    return st.enter_context(SBT(name, shape, dt))

                def ps(name, shape, dt):
                    return PST(name, shape, dt)
                w1 = [sb("c_w1_%d" % kv, [128, 32, 256], BF16) for kv in range(2)]
                Bw1 = [Buf("w1_%d" % kv) for kv in range(2)]
                w2 = sb("c_w2", [128, 2, 2, 128], BF16); Bw2 = Buf("w2")
                xg = [sb("c_xg%d" % i, [128, S], BF16) for i in range(4)]
                Bxg = [Buf("xg%d" % i) for i in range(4)]
                posb = sb("c_pos", [128, 64], BF16); Bpos = Buf("pos")
                pbs = sb("c_pb", [128, 4], F32); Bpb = Buf("pb")
                hid = RPool(st, SBT, "c_hid", 4, [128, 256], BF16)
                sqs = RPool(st, SBT, "c_sq", 2, [128, 256], BF16)
                zbs = RPool(st, SBT, "c_zb", 2, [128, 256], F32)
                rss = RPool(st, SBT, "c_rs", 2, [128, 256], F32)
                pp = RPool(st, ps, "c_ps", 6, [128, 512], F32)
                for kv in range(2):
                    P.dma("pool", DMA(w1[kv][:], cmp_w1[l, kv].rearrange("(i d) h -> d i h", d=128)), Bw1[kv], writes=[Bw1[kv]])
                    P.dma("pool", DMA(w2[:, kv], cmp_w2[l, kv].rearrange("(c p) d -> p c d", p=128)), Bw2, writes=[Bw2])
                    src = SC['kci'] if kv == 0 else SC['vci']
                    for g in range(2):
                        P.dma("sp", DMA(xg[kv * 2 + g][:], src[g * 128:(g + 1) * 128, :]), Bxg[kv * 2 + g], writes=[Bxg[kv * 2 + g]])
                pc = VCOL['cmp_pos']
                P.op("dve", CP(posb[:], vt[:, pc:pc + 64]), reads=[Bvt], writes=[Bpos])
                P.op("dve", lambda e: e.memset(kcT[:], 0.0), writes=[Bkc])
                P.op("pool", lambda e: e.memset(vco[:], 0.0), writes=[Bvco])
                for cc in range(2):
                    for g in range(2):
                        P.dma("pool", DMA(vco[:, cc, g, 128:193], c_ovl[cc * 128:(cc + 1) * 128, :]), Bvco, writes=[Bvco])
                for kv in range(2):
                    for hc in range(2):
                        pt, Bp = pp.next()
                        for i in range(32):
                            P.op("pe", MM(pt[:, 0:1], w1[kv][:, i, hc * 128:(hc + 1) * 128], posb[:, kv * 32 + i:kv * 32 + i + 1], i == 0, i == 31),
                                 reads=[Bw1[kv], Bpos], writes=[Bp], sig=(i == 31))
                        P.op("dve", CP(pbs[:, kv * 2 + hc:kv * 2 + hc + 1], pt[:, 0:1]), reads=[Bp], writes=[Bpb])
                for kv in range(2):
                    for g in range(2):
                        x_ = xg[kv * 2 + g]
                        hts = []
                        for hc in range(2):
                            pt, Bp = pp.next()
                            for i in range(32):
                                P.op("pe", MM(pt[:, 0:255], w1[kv][:, i, hc * 128:(hc + 1) * 128], x_[:, i:i + 16 * 254 + 1:16], i == 0, i == 31),
                                     reads=[Bw1[kv], Bxg[kv * 2 + g]], writes=[Bp], sig=(i == 31))
                            h_, Bh = hid.next()
                            P.op("act", ACTF(h_[:, 0:255], pt[:, 0:255], AF.Silu, bias=pbs[:, kv * 2 + hc:kv * 2 + hc + 1]), reads=[Bp, Bpb], writes=[Bh])
                            hts.append((h_, Bh))
                        if kv == 0:
                            pt, Bp = pp.next()
                            for hc in range(2):
                                P.op("pe", MM(pt[:, 0:255], w2[:, 0, hc, :], hts[hc][0][:, 0:255], hc == 0, hc == 1),
                                     reads=[Bw2, hts[hc][1]], writes=[Bp], sig=(hc == 1))
                            zb, Bz = zbs.next()
                            sq, Bsq = sqs.next()
                            P.op("act", ACTF(zb[:, 0:255], pt[:, 0:255], AF.Identity), reads=[Bp], writes=[Bz])
                            P.op("act", ACTF(sq[:, 0:255], pt[:, 0:255], AF.Square), reads=[Bp], writes=[Bsq])
                            pq, Bpq = pp.next()
                            P.op("pe", MM(pq[:, 0:255], om128_b[:], sq[:, 0:255], True, True), reads=[Bsq, Bc], writes=[Bpq])
                            rs, Brs = rss.next()
                            P.op("act", ACTF(rs[:, 0:255], pq[:, 0:255], AF.Ln, bias=epsb[:, 0:1]), reads=[Bpq, Bc], writes=[Brs])
                            P.op("act", ACTF(rs[:, 0:255], rs[:, 0:255], AF.Exp, scale=-0.5), reads=[Brs], writes=[Brs])
                            gk = VCOL['nsa_k_gain']
                            P.op("dve", STT(kcT[:, g, 0:255], zb[:, 0:255], vt[:, gk:gk + 1], rs[:, 0:255], ALU.mult, ALU.mult),
                                 reads=[Bz, Bvt, Brs], writes=[Bkc])
                        else:
                            for cc in range(2):
                                n = 128 if cc == 0 else 127
                                pt, Bp = pp.next()
                                for hc in range(2):
                                    P.op("pe", MM(pt[:n, 0:128], hts[hc][0][:, cc * 128:cc * 128 + n], w2[:, 1, hc, :], hc == 0, hc == 1),
                                         reads=[Bw2, hts[hc][1]], writes=[Bp], sig=(hc == 1))
                                P.op("dve", CP(vco[:n, cc, g, 0:128], pt[:n, 0:128]), reads=[Bp], writes=[Bvco])
                if 'kcd' in dbg:
                    P.dma("sp", DMA(SC['kcd'], kcT[:]), Bkc, reads=[Bkc])
                    P.dma("sp", DMA(SC['vcd'], vco[:]), Bvco, reads=[Bvco])
            P.new_phase()


        def phase4(l, qts=range(32)):
            with ExitStack() as st:
                def sb(name, shape, dt):
                    return st.enter_context(SBT(name, shape, dt))
                NEG = 30000.0
                ksT = sb("n_ks", [128, 2, S], BF16); Bks = Buf("ks")
                kwT = sb("n_kw", [128, 2, S], BF16); Bkw = Buf("kw")
                vs = sb("n_vs", [128, 32, 2, 130], BF16); Bvs = Buf("vs")
                vw = sb("n_vw", [128, 32, 2, 130], BF16); Bvw = Buf("vw")
                bgt = sb("n_bg", [128, 32, 24], F32); Bbg = Buf("bg")
                cmk = sb("n_cmk", [128, 17, 128], BF16); Bcmk = Buf("cmk")
                tkm = sb("n_tkm", [128, 32, 64], F32); Btkm = Buf("tkm")
                tka = sb("n_tka", [128, 32, 64], F32); Btka = Buf("tka")
                esl = sb("n_esl", [64, 32, 128], BF16); Besl = Buf("esl")
                qp = RPool(st, SBT, "n_q", 3, [128, 8, 128], BF16)
                gp = RPool(st, SBT, "n_g", 3, [128, 8, 128], BF16)
                ptp = RPool(st, SBT, "n_pt", 6, [128, 512], BF16)
                yp = RPool(st, SBT, "n_y", 2, [128, 4, 128], F32)
                ybp = RPool(st, SBT, "n_yb", 2, [128, 4, 128], BF16)
                yop = RPool(st, SBT, "n_yo", 3, [128, 4, 128], BF16)
                smp = RPool(st, SBT, "n_sm", 2, [128, 64 * 4 + 32], F32)
                sbp = RPool(st, SBT, "n_sb", 2, [128, 64], BF16)
                s4p = RPool(st, SBT, "n_s4", 2, [64, 4, 128], BF16)
                stp = RPool(st, PST, "n_st", 3, [128, 512], F32)
                acp = RPool(st, PST, "n_ac", 2, [128, 4, 256], F32)
                mip = RPool(st, PST, "n_mi", 1, [128, 512], BF16)
                for g in range(2):
                    P.dma("sp", DMA(ksT[:, g, :], SC['ksl'][g * 128:(g + 1) * 128, :]), Bks, writes=[Bks])
                    P.dma("sp", DMA(kwT[:, g, :], SC['kwn'][g * 128:(g + 1) * 128, :]), Bkw, writes=[Bkw])
                    P.dma("sp", DMA(vs[:, :, g, 0:128], SC['vsl'][:, g * 128:(g + 1) * 128].rearrange("(k p) d -> p k d", p=128)), Bvs, writes=[Bvs])
                    P.dma("sp", DMA(vw[:, :, g, 0:128], SC['vwn'][:, g * 128:(g + 1) * 128].rearrange("(k p) d -> p k d", p=128)), Bvw, writes=[Bvw])
                P.op("dve", lambda e: e.memset(vs[:, :, :, 128:129], 1.0), writes=[Bvs])
                P.op("dve", lambda e: e.memset(vw[:, :, :, 128:129], 1.0), writes=[Bvw])
                P.dma("sp", DMA(bgt[:], SC['bg'].rearrange("(k p) c -> p k c", p=128)), Bbg, writes=[Bbg])
                P.dma("pool", DMA(cmk[:], c_cmask), Bcmk, writes=[Bcmk])
                P.dma("sp", DMA(tkm[:], c_tkmul), Btkm, writes=[Btkm])
                P.dma("sp", DMA(tka[:], c_tkadd), Btka, writes=[Btka])
                P.dma("pool", DMA(esl[:], c_esel), Besl, writes=[Besl])
                qn_v = SC['qn'].rearrange("(h d) t -> d h t", d=128)
                ng_v = SC['ng'].rearrange("(h d) t -> d h t", d=128)
                yb_v = SC['ybT'].rearrange("(h d) t -> d h t", d=128)

                def attend(acc, Bacc, keysT, Bk, vals, Bv, g, q_, Bq, kcs, qt, bias=None, vw_=129, pre=None):
                    kcs = list(kcs)
                    LA = 2
                    pend = {}

                    def stage1(n_):
                        kc = kcs[n_]
                        s_, Bs = stp.next()
                        rhs = q_[:, 4 * g:4 * g + 4, :]
                        P.op("pe", MM(s_[:], keysT[:, g, kc * 128:(kc + 1) * 128], rhs, True, bias is None), reads=[Bk, Bq], writes=[Bs], sig=(bias is None))
                        if bias is not None:
                            P.op("pe", MM(s_[:], esl[:, kc, :], bias[0][:].rearrange("p a b -> p (a b)"), False, True), reads=[Besl, bias[1]], writes=[Bs])
                        p_, Bp = ptp.next()
                        P.op("act", ACTF(p_[:], s_[:], AF.Exp), reads=[Bs], writes=[Bp])
                        m = pre(kc) if pre is not None else None
                        if m is not None:
                            pv = p_[:].rearrange("p (a b) -> p a b", a=4)
                            P.op("pool", TT(pv, pv, m[0].unsqueeze(1).to_broadcast([128, 4, 128]), ALU.mult), reads=[Bp, m[1]], writes=[Bp])
                        return p_, Bp

                    def stage2(n_, p_, Bp):
                        kc = kcs[n_]
                        for hh in range(4):
                            P.op("pe", MM(acc[:, hh, 0:vw_], p_[:, hh * 128:(hh + 1) * 128], vals(kc), n_ == 0 and hh % 2 == 0, n_ == len(kcs) - 1, skip=True),
                                 reads=[Bp, Bv], writes=[Bacc], sig=(hh == 3))
                    for n_ in range(len(kcs) + LA):
                        if n_ < len(kcs):
                            pend[n_] = stage1(n_)
                        if n_ - LA >= 0:
                            stage2(n_ - LA, *pend.pop(n_ - LA))

                deferred = []
                for qt in qts:
                    q0 = qt * 128
                    q_, Bq = qp.next()
                    P.dma("sp", DMA(q_[:], qn_v[:, :, q0:q0 + 128]), Bq, writes=[Bq])
                    gt_, Bg = gp.next()
                    P.dma("sp", DMA(gt_[:], ng_v[:, :, q0:q0 + 128]), Bg, writes=[Bg])
                    for g in range(2):
                        sm, Bsm = smp.next()
                        imp = sm[:, 0:64]
                        sco = sm[:, 64:128]
                        wrk = sm[:, 128:192]
                        selm = sm[:, 192:256]
                        mx8 = sm[:, 256:264]
                        rd = sm[:, 264:268]
                        cf = sm[:, 268:272]
                        rd2 = sm[:, 272:276]
                        cf2 = sm[:, 276:280]
                        bgv = bgt[:, qt, :].rearrange("p (h c) -> p h c", c=3)
                        y_, By = yp.next()
                        accA, BaA = acp.next()
                        ncc = 1 if qt < 16 else 2

                        def preA(cc, qt=qt):
                            di = qt - 16 * cc
                            return (cmk[:, di, :], Bcmk) if di <= 16 else None
                        attend(accA, BaA, kcT, Bkc, lambda cc, g=g: vco[:, cc, g, :], Bvco, g, q_, Bq, range(ncc), qt, vw_=193, pre=preA)
                        P.op("dve", TS(rd, accA[:, :, 192], 1e-30, None, ALU.max), reads=[BaA], writes=[Bsm])
                        P.op("dve", lambda e, rd=rd: e.reciprocal(out=rd, in_=rd), reads=[Bsm], writes=[Bsm])
                        P.op("dve", TS(imp, accA[:, 0, 128:192], rd[:, 0:1], None, ALU.mult), reads=[BaA, Bsm], writes=[Bsm])
                        for hh in range(1, 4):
                            P.op("dve", STT(imp, accA[:, hh, 128:192], rd[:, hh:hh + 1], imp, ALU.mult, ALU.add), reads=[BaA, Bsm], writes=[Bsm])
                        P.op("dve", TT(sco, imp, tkm[:, qt, :], ALU.mult), reads=[Bsm, Btkm], writes=[Bsm])
                        P.op("dve", TT(sco, sco, tka[:, qt, :], ALU.add), reads=[Bsm, Btka], writes=[Bsm])
                        P.op("dve", lambda e, mx8=mx8, sco=sco: e.max(out=mx8, in_=sco), reads=[Bsm], writes=[Bsm])
                        P.op("dve", lambda e, wrk=wrk, mx8=mx8, sco=sco: e.match_replace(out=wrk, in_to_replace=mx8, in_values=sco, imm_value=-3e6), reads=[Bsm], writes=[Bsm])
                        P.op("dve", lambda e, mx8=mx8, wrk=wrk: e.max(out=mx8, in_=wrk), reads=[Bsm], writes=[Bsm])
                        P.op("dve", lambda e, wrk=wrk, mx8=mx8: e.match_replace(out=wrk, in_to_replace=mx8, in_values=wrk, imm_value=-3e6), reads=[Bsm], writes=[Bsm])
                        P.op("dve", TT(selm, sco, wrk, ALU.subtract), reads=[Bsm], writes=[Bsm])
                        P.op("dve", TS(selm, selm, 1.0, None, ALU.min), reads=[Bsm], writes=[Bsm])
                        sb_, Bsb = sbp.next()
                        P.op("dve", TS(sb_[:], selm, -1.0, NEG, ALU.add, ALU.mult), reads=[Bsm], writes=[Bsb])
                        P.op("dve", TT(cf, bgv[:, 4 * g:4 * g + 4, 0], rd, ALU.mult), reads=[Bbg, Bsm], writes=[Bsm])
                        for hh in range(4):
                            P.op("dve", TS(y_[:, hh, :], accA[:, hh, 0:128], cf[:, hh:hh + 1], None, ALU.mult), reads=[BaA, Bsm], writes=[By])
                        accC, BaC = acp.next()

                        def preC(kc, qt=qt):
                            if kc == qt:
                                return (tri_b[:], Bc)
                            if kc == qt - 4:
                                return (triu_b[:], Bc)
                            return None
                        attend(accC, BaC, kwT, Bkw, lambda kc, g=g: vw[:, kc, g, 0:129], Bvw, g, q_, Bq, range(max(0, qt - 4), qt + 1), qt, pre=preC)
                        while deferred:
                            deferred.pop(0)()
                        mi, Bmi = mip.next()
                        P.op("pe", TR(mi[0:64, 0:128], sb_[:], ident_b[:]), reads=[Bsb, Bc], writes=[Bmi])
                        s4, Bs4 = s4p.next()
                        P.op("dve", CP(s4[:], mi[0:64, 0:128].unsqueeze(1).to_broadcast([64, 4, 128])), reads=[Bmi], writes=[Bs4])
                        P.op("dve", lambda e, rd2=rd2, accC=accC: e.reciprocal(out=rd2, in_=accC[:, :, 128]), reads=[BaC], writes=[Bsm])
                        P.op("dve", TT(cf2, bgv[:, 4 * g:4 * g + 4, 2], rd2, ALU.mult), reads=[Bbg, Bsm], writes=[Bsm])
                        for hh in range(4):
                            P.op("dve", STT(y_[:, hh, :], accC[:, hh, 0:128], cf2[:, hh:hh + 1], y_[:, hh, :], ALU.mult, ALU.add), reads=[BaC, Bsm, By], writes=[By])
                        accB, BaB = acp.next()
                        attend(accB, BaB, ksT, Bks, lambda kc, g=g: vs[:, kc, g, 0:129], Bvs, g, q_, Bq, range(qt + 1), qt, bias=(s4, Bs4),
                               pre=lambda kc, qt=qt: (tri_b[:], Bc) if kc == qt else None)
                        P.op("dve", lambda e, rd=rd, accB=accB: e.reciprocal(out=rd, in_=accB[:, :, 128]), reads=[BaB], writes=[Bsm])
                        P.op("dve", TT(cf, bgv[:, 4 * g:4 * g + 4, 1], rd, ALU.mult), reads=[Bbg, Bsm], writes=[Bsm])
                        yb_, Byb = ybp.next()
                        for hh in range(4):
                            P.op("dve", STT(yb_[:, hh, :], accB[:, hh, 0:128], cf[:, hh:hh + 1], y_[:, hh, :], ALU.mult, ALU.add), reads=[BaB, Bsm, By], writes=[Byb])

                        def emitD(yb_=yb_, Byb=Byb, gt_=gt_, Bg=Bg, g=g, q0=q0):
                            mi, Bmi = mip.next()
                            for hh in range(4):
                                P.op("pe", TR(mi[:, hh * 128:(hh + 1) * 128], yb_[:, hh, :], ident_b[:]), reads=[Byb, Bc], writes=[Bmi], sig=(hh == 3))
                            yo, Byo = yop.next()
                            P.op("dve", TT(yo[:], mi[:].rearrange("p (a b) -> p a b", a=4), gt_[:, 4 * g:4 * g + 4, :], ALU.mult), reads=[Bmi, Bg], writes=[Byo])
                            P.dma("sp", DMA(yb_v[:, 4 * g:4 * g + 4, q0:q0 + 128], yo[:]), Byo, reads=[Byo])
                        deferred.append(emitD)
                while deferred:
                    deferred.pop(0)()
            P.new_phase()

        def phase5(l, heads=range(8), qss=range(8)):
            with ExitStack() as st:
                def sb(name, shape, dt):
                    return st.enter_context(SBT(name, shape, dt))
                fft = sb("f_ff", [8, S], F32); Bff = Buf("ff")
                onesr = sb("f_ones", [8, S], F32); Bon = Buf("ones")
                cum = sb("f_cum", [8, S], F32); Bcum = Buf("cum")
                dd = sb("f_dd", [8, 8, 8], F32); Bdd = Buf("dd")
                ones8 = sb("f_ones8", [8, 128], F32)
                ctok = sb("f_ctok", [128, 32, 8], F32); Bct = Buf("ctok")
                crefb = sb("f_cref", [128, 8, 8], F32); Bcr = Buf("cref")
                qh = RPool(st, SBT, "f_q", 2, [128, S], BF16)
                kh = RPool(st, SBT, "f_k", 2, [128, S], BF16)
                vh = RPool(st, SBT, "f_v", 2, [128, 32, 130], BF16)
                gp = RPool(st, SBT, "f_g", 3, [128, 512], BF16)
                btp = RPool(st, SBT, "f_bt", 3, [128, 32], F32)
                ptp = RPool(st, SBT, "f_pt", 8, [128, 512], BF16)
                rdp = RPool(st, SBT, "f_rd", 3, [128, 512], F32)
                ybp = RPool(st, SBT, "f_yb", 2, [128, 4, 128], BF16)
                yop = RPool(st, SBT, "f_yo", 3, [128, 512], BF16)
                stp = RPool(st, PST, "f_st", 4, [128, 512], F32)
                acp = RPool(st, PST, "f_ac", 2, [128, 2, 512], F32)
                P.dma("sp", DMA(fft[:], SC['ff']), Bff, writes=[Bff])
                P.op("dve", lambda e: e.memset(onesr[:], 1.0), writes=[Bon])
                P.op("dve", lambda e: e.memset(ones8[:], 1.0), writes=[Bon])
                P.op("act", ACTF(fft[:], fft[:], AF.Exp, scale=-1.0), reads=[Bff], writes=[Bff])
                P.op("act", ACTF(fft[:], fft[:], AF.Ln, bias=one_f[0:8, 0:1]), reads=[Bff, Bc], writes=[Bff])
                P.op("dve", lambda e: e.tensor_tensor_scan(out=cum[:], data0=onesr[:], data1=fft[:], initial=0.0, op0=ALU.mult, op1=ALU.add),
                     reads=[Bon, Bff], writes=[Bcum])
                if 'cf' in dbg:
                    P.dma("sp", DMA(SC['cf'], cum[:]), Bcum, reads=[Bcum])
                mt, Bmt = stp.next()
                for kc in range(32):
                    P.op("pe", TR(mt[:, kc * 8:(kc + 1) * 8], cum[:, kc * 128:(kc + 1) * 128], ident_f[0:8, 0:8]), reads=[Bcum, Bc], writes=[Bmt], sig=(kc == 31))
                P.op("dve", CP(ctok[:].rearrange("p k h -> p (k h)"), mt[:, 0:256]), reads=[Bmt], writes=[Bct])
                P.op("dve", lambda e: e.memset(dd[:], 0.0), writes=[Bdd])
                P.op("dve", TT(dd[:, :, 1:8], ident_f[0:8, 0:8].unsqueeze(2).to_broadcast([8, 8, 7]),
                               cum[:, 511:511 + 512 * 6 + 1:512].unsqueeze(1).to_broadcast([8, 8, 7]), ALU.mult), reads=[Bc, Bcum], writes=[Bdd])
                mt2, Bmt2 = stp.next()
                P.op("pe", MM(mt2[:, 0:64], ones8[:], dd[:].rearrange("p a b -> p (a b)"), True, True), reads=[Bon, Bdd], writes=[Bmt2])
                P.op("dve", CP(crefb[:].rearrange("p a b -> p (a b)"), mt2[:, 0:64]), reads=[Bmt2], writes=[Bcr])
                for h in heads:
                    q_, Bq = qh.next()
                    k_, Bk = kh.next()
                    v_, Bv = vh.next()
                    P.dma("sp", DMA(q_[:], SC['fq'][h * 128:(h + 1) * 128, :]), Bq, writes=[Bq])
                    P.dma("sp", DMA(k_[:], SC['fk'][h * 128:(h + 1) * 128, :]), Bk, writes=[Bk])
                    P.dma("sp", DMA(v_[:, :, 0:128], SC['fv'][:, h * 128:(h + 1) * 128].rearrange("(k p) d -> p k d", p=128)), Bv, writes=[Bv])
                    P.op("pool", lambda e, v_=v_: e.memset(v_[:, :, 128:129], 1.0), writes=[Bv])
                    for qs in qss:
                        q0 = qs * 512
                        nk = 4 * qs + 4
                        bt, Bbt = btp.next()
                        P.op("dve", TS(bt[:, 0:nk], ctok[:, 0:nk, h], crefb[:, h, qs:qs + 1], None, ALU.subtract), reads=[Bct, Bcr], writes=[Bbt])
                        g_, Bg = gp.next()
                        P.dma("sp", DMA(g_[:], SC['fg'][h * 128:(h + 1) * 128, q0:q0 + 512]), Bg, writes=[Bg])
                        acc, Bacc = acp.next()
                        LA = 3
                        pend = {}

                        def stage1(kc, qs=qs, q0=q0, q_=q_, Bq=Bq, k_=k_, Bk=Bk, bt=bt, Bbt=Bbt):
                            j = kc - 4 * qs
                            c0 = max(0, j) * 128
                            s_, Bs = stp.next()
                            P.op("pe", MM(s_[:, c0:512], k_[:, kc * 128:(kc + 1) * 128], q_[:, q0 + c0:q0 + 512], True, True), reads=[Bk, Bq], writes=[Bs])
                            p_, Bp = ptp.next()
                            P.op("act", ACTF(p_[:, c0:512], s_[:, c0:512], AF.Exp, bias=bt[:, kc:kc + 1]), reads=[Bs, Bbt], writes=[Bp])
                            if j >= 0:
                                P.op("pool", TT(p_[:, c0:c0 + 128], p_[:, c0:c0 + 128], tri_b[:], ALU.mult), reads=[Bp, Bc], writes=[Bp])
                            return p_, Bp

                        def stage2(kc, p_, Bp, qs=qs, acc=acc, Bacc=Bacc, v_=v_, Bv=Bv, nk=nk):
                            c0 = max(0, kc - 4 * qs) * 128
                            P.op("pe", MM(acc[:, 0, c0:512], v_[:, kc, 0:128], p_[:, c0:512], kc == 0, kc == nk - 1), reads=[Bp, Bv], writes=[Bacc], sig=False)
                            P.op("pe", MM(acc[:, 1, c0:512], ones_b[:], p_[:, c0:512], kc == 0, kc == nk - 1), reads=[Bp, Bc], writes=[Bacc])
                        for kc in range(nk + LA):
                            if kc < nk:
                                pend[kc] = stage1(kc)
                            if kc - LA >= 0:
                                stage2(kc - LA, *pend.pop(kc - LA))
                        rd, Brd = rdp.next()
                        P.op("dve", lambda e, rd=rd, acc=acc: e.reciprocal(out=rd[:], in_=acc[:, 1, :]), reads=[Bacc], writes=[Brd])
                        P.op("dve", TT(rd[:], rd[:], acc[:, 0, :], ALU.mult), reads=[Bacc, Brd], writes=[Brd])
                        yo, Byo = yop.next()
                        P.op("pool", TT(yo[:], rd[:], g_[:], ALU.mult), reads=[Brd, Bg], writes=[Byo])
                        P.dma("sp", DMA(SC['ycT'][h * 128:(h + 1) * 128, q0:q0 + 512], yo[:]), Byo, reads=[Byo])
            P.new_phase()


        def phase6(l, halves=range(2)):
            for hf in halves:
                t0 = hf * 2048
                with ExitStack() as st:
                    def sb(name, shape, dt):
                        return st.enter_context(SBT(name, shape, dt))
                    yy = [sb("m_y%d" % i, [128, 8, 2048], BF16) for i in range(3)]
                    Byy = [[Buf("y%d_%d" % (i, t)) for t in range(4)] for i in range(3)]
                    wbp = RPool(st, SBT, "m_wb", 9, [128, 8, 128], BF16)
                    mgp = RPool(st, SBT, "m_mg", 6, [128, 2048], BF16)
                    m1p = RPool(st, SBT, "m_m1", 6, [128, 512], F32)
                    mop = RPool(st, SBT, "m_mo", 3, [128, 512], BF16)
                    pp = RPool(st, PST, "m_ps", 6, [128, 512], F32)
                    for t in range(4):
                        for i, nm in enumerate(('yaT', 'ybT', 'ycT')):
                            P.dma("sp", DMA(yy[i][:, :, t * 512:(t + 1) * 512], SC[nm][:, t0 + t * 512:t0 + (t + 1) * 512].rearrange("(c p) t -> p c t", p=128)),
                                  Byy[i][t], writes=[Byy[i][t]])
                    for f in range(16):
                        ws, ms = [], []
                        for br in range(3):
                            w_, Bw = wbp.next()
                            P.dma("pool", DMA(w_[:], w_branch[l, br, f]), Bw, writes=[Bw])
                            ws.append((w_, Bw))
                            m_, Bm = mgp.next()
                            r0 = br * 2048 + f * 128
                            P.dma("sp", DMA(m_[:], SC['mg'][r0:r0 + 128, t0:t0 + 2048]), Bm, writes=[Bm])
                            ms.append((m_, Bm))
                        for t in range(4):
                            ts_ = slice(t * 512, (t + 1) * 512)
                            parts = []
                            for br in range(3):
                                pt, Bp = pp.next()
                                for c in range(8):
                                    P.op("pe", MM(pt[:], ws[br][0][:, c, :], yy[br][:, c, ts_], c == 0, c == 7), reads=[ws[br][1], Byy[br][t]], writes=[Bp], sig=(c == 7))
                                m1, Bm1 = m1p.next()
                                P.op("dve", TT(m1[:], pt[:], ms[br][0][:, ts_], ALU.mult), reads=[Bp, ms[br][1]], writes=[Bm1])
                                parts.append((m1, Bm1))
                            P.op("dve", TT(parts[0][0][:], parts[0][0][:], parts[1][0][:], ALU.add), reads=[parts[0][1], parts[1][1]], writes=[parts[0][1]])
                            mo, Bmo = mop.next()
                            P.op("dve", TT(mo[:], parts[0][0][:], parts[2][0][:], ALU.add), reads=[parts[0][1], parts[2][1]], writes=[Bmo])
                            P.dma("sp", DMA(SC['mgd'][f * 128:(f + 1) * 128, t0 + t * 512:t0 + (t + 1) * 512], mo[:]), Bmo, reads=[Bmo])
                P.new_phase()
                with ExitStack() as st2:
                    hmb = st2.enter_context(SBT("m_hmb", [128, 16, 2048], BF16))
                    Bhm = [[Buf("hmb%d_%d" % (f, t)) for t in range(4)] for f in range(16)]
                    with ExitStack() as st:
                        def sb(name, shape, dt):
                            return st.enter_context(SBT(name, shape, dt))
                        mgd = sb("m_mgd", [128, 16, 2048], BF16); Bmgd = [Buf("mgd%d" % t) for t in range(4)]
                        wop = RPool(st, SBT, "m_wo", 3, [128, 16, 128], BF16)
                        hip = RPool(st, SBT, "m_hi", 3, [128, 512], F32)
                        hop = RPool(st, SBT, "m_ho", 3, [128, 512], F32)
                        pp = RPool(st, PST, "m_pb", 4, [128, 512], F32)
                        for t in range(4):
                            P.dma("sp", DMA(mgd[:, :, t * 512:(t + 1) * 512], SC['mgd'][:, t0 + t * 512:t0 + (t + 1) * 512].rearrange("(c p) t -> p c t", p=128)),
                                  Bmgd[t], writes=[Bmgd[t]])
                        preb = {}
                        for f in range(16):
                            w_, Bw = wop.next()
                            P.dma("pool", DMA(w_[:], w_out[l, f]), Bw, writes=[Bw])
                            for t in range(4):
                                ts_ = slice(t * 512, (t + 1) * 512)
                                hsl = SC['hT'][f * 128:(f + 1) * 128, t0 + t * 512:t0 + (t + 1) * 512]

                                def ldb(k, pre=preb):
                                    if k < 64 and k not in pre:
                                        f_, t_ = k // 4, k % 4
                                        hi_, Bhi_ = hip.next()
                                        P.dma("sp", DMA(hi_[:], SC['hT'][f_ * 128:(f_ + 1) * 128, t0 + t_ * 512:t0 + (t_ + 1) * 512]), Bhi_, writes=[Bhi_])
                                        pre[k] = (hi_, Bhi_)
                                ldb(f * 4 + t)
                                ldb(f * 4 + t + 1)
                                hi, Bhi = preb.pop(f * 4 + t)
                                pt, Bp = pp.next()
                                for c in range(16):
                                    P.op("pe", MM(pt[:], w_[:, c, :], mgd[:, c, ts_], c == 0, c == 15), reads=[Bw, Bmgd[t]], writes=[Bp], sig=(c == 15))
                                ho, Bho = hop.next()
                                P.op("dve", TT(ho[:], pt[:], hi[:], ALU.add), reads=[Bp, Bhi], writes=[Bho])
                                P.op("act", ACTF(hmb[:, f, ts_], ho[:], AF.Identity), reads=[Bho], writes=[Bhm[f][t]])
                                P.dma("sp", DMA(hsl, ho[:]), Bho, reads=[Bho])
                    P.new_phase()
                    with ExitStack() as st:
                        def sb(name, shape, dt):
                            return st.enter_context(SBT(name, shape, dt))
                        wpl = sb("e_wpl", [128, 2, D], BF16); Bwpl = Buf("wpl")
                        pT = sb("e_pT", [128, 2, 2048], BF16); BpT = [Buf("pT%d" % i) for i in range(16)]
                        eall = sb("e_all", [128, 16, 2048], BF16); Bea = [Buf("ea%d" % i) for i in range(8)]
                        epre = sb("e_pre", [128, 16, 256], F32); Bep = Buf("epre")
                        pip = RPool(st, SBT, "e_pi", 3, [128, 256], F32)
                        sqp = RPool(st, SBT, "e_sq", 4, [128, 256], BF16)
                        rsp = RPool(st, SBT, "e_rs", 2, [128, 256], F32)
                        wgp = RPool(st, SBT, "e_wg", 3, [128, 16, 128], BF16)
                        hip = RPool(st, SBT, "e_hi", 2, [128, 512], F32)
                        sgp = RPool(st, SBT, "e_sg", 2, [128, 512], F32)
                        hop = RPool(st, SBT, "e_ho", 2, [128, 512], F32)
                        pp = RPool(st, PST, "e_ps", 4, [128, 512], F32)
                        p2 = RPool(st, PST, "e_p2", 2, [128, 512], F32)
                        P.dma("pool", DMA(wpl[:], w_ple[l].rearrange("(c p) n -> p c n", p=128)), Bwpl, writes=[Bwpl])
                        for tt in range(16):
                            pi, Bpi = pip.next()
                            r0 = t0 + tt * 128
                            P.dma("sp", DMA(pi[:], p_in[l, r0:r0 + 128, :]), Bpi, writes=[Bpi])
                            pt, Bp = pp.next()
                            for c in range(2):
                                P.op("pe", TR(pt[:, c * 128:(c + 1) * 128], pi[:, c * 128:(c + 1) * 128], ident_f[:]), reads=[Bpi, Bc], writes=[Bp], sig=(c == 1))
                            P.op("dve", CP(pT[:, :, tt * 128:(tt + 1) * 128], pt[:, 0:256].rearrange("p (c t) -> p c t", c=2)), reads=[Bp], writes=[BpT[tt]])
                        pg = VCOL['ple_gain']
                        for t in range(8):
                            ts_ = slice(t * 256, (t + 1) * 256)
                            pq, Bpq = p2.next()
                            pend = []
                            for f in range(16):
                                pt, Bp = pp.next()
                                for c in range(2):
                                    P.op("pe", MM(pt[:, 0:256], wpl[:, c, f * 128:(f + 1) * 128], pT[:, c, ts_], c == 0, c == 1),
                                         reads=[Bwpl, BpT[2 * t], BpT[2 * t + 1]], writes=[Bp], sig=(c == 1))
                                while len(pend) > 1:
                                    pend.pop(0)()
                                P.op("act", ACTF(epre[:, f, :], pt[:, 0:256], AF.Identity), reads=[Bp], writes=[Bep])
                                sq, Bsq = sqp.next()
                                P.op("act", ACTF(sq[:], pt[:, 0:256], AF.Square), reads=[Bp], writes=[Bsq])
                                pend.append(lambda sq=sq, Bsq=Bsq, f=f, pq=pq, Bpq=Bpq: P.op("pe", MM(pq[:, 0:256], om2048_b[:], sq[:], f == 0, f == 15), reads=[Bsq, Bc], writes=[Bpq]))
                            while pend:
                                pend.pop(0)()
                            rs, Brs = rsp.next()
                            P.op("act", ACTF(rs[:], pq[:, 0:256], AF.Ln, bias=epsb[:, 0:1]), reads=[Bpq, Bc], writes=[Brs])
                            P.op("act", ACTF(rs[:], rs[:], AF.Exp, scale=-0.5), reads=[Brs], writes=[Brs])
                            for f in range(16):
                                P.op("dve", STT(eall[:, f, ts_], epre[:, f, :], vt[:, pg + f:pg + f + 1], rs[:], ALU.mult, ALU.mult),
                                     reads=[Bep, Bvt, Brs], writes=[Bea[t]])
                        prec = {}
                        for f in range(16):
                            w_, Bw = wgp.next()
                            P.dma("pool", DMA(w_[:], w_ple_gate[l, f]), Bw, writes=[Bw])
                            for t in range(4):
                                ts_ = slice(t * 512, (t + 1) * 512)
                                hsl = SC['hT'][f * 128:(f + 1) * 128, t0 + t * 512:t0 + (t + 1) * 512]

                                def ldc(k, pre=prec):
                                    if k < 64 and k not in pre:
                                        f_, t_ = k // 4, k % 4
                                        hi_, Bhi_ = hip.next()
                                        P.dma("sp", DMA(hi_[:], SC['hT'][f_ * 128:(f_ + 1) * 128, t0 + t_ * 512:t0 + (t_ + 1) * 512]), Bhi_, writes=[Bhi_])
                                        pre[k] = (hi_, Bhi_)
                                ldc(f * 4 + t)
                                ldc(f * 4 + t + 1)
                                hi, Bhi = prec.pop(f * 4 + t)
                                pt, Bp = pp.next()
                                for c in range(16):
                                    P.op("pe", MM(pt[:], w_[:, c, :], hmb[:, c, ts_], c == 0, c == 15), reads=[Bw, Bhm[c][t]], writes=[Bp], sig=(c == 15))
                                sg, Bsg = sgp.next()
                                P.op("act", ACTF(sg[:], pt[:], AF.Sigmoid), reads=[Bp], writes=[Bsg])
                                P.op("dve", TT(sg[:], sg[:], eall[:, f, ts_], ALU.mult), reads=[Bsg, Bea[2 * t], Bea[2 * t + 1]], writes=[Bsg])
                                ho, Bho = hop.next()
                                P.op("dve", TT(ho[:], sg[:], hi[:], ALU.add), reads=[Bsg, Bhi], writes=[Bho])
                                P.dma("sp", DMA(hsl, ho[:]), Bho, reads=[Bho])
                    P.new_phase()

        def phase7():
            with ExitStack() as st:
                hp = RPool(st, SBT, "o_h", 20, [128, 512], F32)
                yp = RPool(st, SBT, "o_y", 2, [128, D], F32)
                pp = RPool(st, PST, "o_ps", 4, [128, 512], F32)
                for g in range(8):
                    hc = []
                    for c in range(16):
                        h_, Bh = hp.next()
                        P.dma("sp", DMA(h_[:], SC['hT'][c * 128:(c + 1) * 128, g * 512:(g + 1) * 512]), Bh, writes=[Bh])
                        hc.append((h_, Bh))
                    for j in range(4):
                        yt, By = yp.next()
                        for cq in range(4):
                            pt, Bp = pp.next()
                            for k in range(4):
                                c = cq * 4 + k
                                P.op("pe", TR(pt[:, k * 128:(k + 1) * 128], hc[c][0][:, j * 128:(j + 1) * 128], ident_f[:]), reads=[hc[c][1], Bc], writes=[Bp], sig=(k == 3))
                            if cq % 2 == 0:
                                P.op("act", ACTF(yt[:, cq * 512:(cq + 1) * 512], pt[:], AF.Identity), reads=[Bp], writes=[By])
                            else:
                                P.op("dve", CP(yt[:, cq * 512:(cq + 1) * 512], pt[:]), reads=[Bp], writes=[By])
                        r0 = (g * 4 + j) * 128
                        P.dma("sp", DMA(y[r0:r0 + 128, :], yt[:]), By, reads=[By])
            P.new_phase()

        if 0 not in skip:
            phase0()
        if stop_after == 'p0':
            P.emit()
            return nc
        for l in range(n_layers):
            if 1 not in skip:
                phase1(l)
            else:
                P.dma("sp", DMA(vt[:], vec[l]), Bvt, writes=[Bvt])
                P.new_phase()
            if stop_after == 'p1':
                break
            if phases is None or 2 in phases:
                phase2(l)
            if phases is None or 3 in phases:
                phase3(l)
            if stop_after == 'p3':
                break
            if phases is None or 4 in phases:
                phase4(l, qts=(range(32) if qts is None else qts))
            if stop_after == 'p4':
                break
            if phases is None or 5 in phases:
                phase5(l, **fox_kw)
            if stop_after == 'p5':
                break
            if phases is None or 6 in phases:
                phase6(l, **p6_kw)
            if stop_after == 'p6':
                break
        if phases is None or 7 in phases:
            phase7()
        P.emit()
    return nc


def make_in_maps(inp, cores=range(8)):
    vec, rows = host_vecs(inp)
    cst = host_consts()
    def tile_sq(w):
        k = w.shape[1] // 128
        return np.ascontiguousarray(w.reshape(L, k, 128, 16, 128).transpose(0, 3, 2, 1, 4))
    wout_t = tile_sq(inp['w_out'])
    wg_t = tile_sq(inp['w_ple_gate'])
    wbr_t = np.ascontiguousarray(inp['w_branch'].reshape(L, 3, 8, 128, 16, 128).transpose(0, 1, 4, 3, 2, 5))
    maps = []
    for b in cores:
        m = dict(x=np.ascontiguousarray(inp['x'][b]), p=np.ascontiguousarray(inp['p'][:, b]),
                 w_in=inp['w_in'], vec=vec, rows=rows,
                 lru_wa=inp['lru_wa'], lru_wx=inp['lru_wx'], cmp_w1=inp['cmp_w1'], cmp_w2=inp['cmp_w2'],
                 w_branch=wbr_t, w_out=wout_t, w_ple=inp['w_ple'], w_ple_gate=wg_t,
                 c_ident=cst['ident'], c_tri=cst['tri'], c_triu=cst['triu'], c_ovl=cst['ovl'], c_cmask=cst['cmask'],
                 c_tkmul=cst['tkmul'], c_tkadd=cst['tkadd'], c_esel=cst['esel'])
        maps.append(m)
    return maps


_NC = None


def kernel(**inp):
    global _NC
    inp = {k: np.asarray(v) for k, v in inp.items()}
    if _NC is None:
        _NC = build()
    res = run_bass_kernel_spmd(_NC, make_in_maps(inp), core_ids=list(range(8)))
    return np.stack([np.asarray(r["y"]) for r in res.results], axis=0)
```
